# Optimizing a Trainium2 kernel written in Bass

```python
import jax, jax.numpy as jnp
from jax import lax
import numpy as np

D_MODEL = 2048
BATCH = 2
SEQ = 4096
DEPTH = 1
DEC_BATCH = 16
DEC_SEQ = 32
PAST_LEN = 1024

CHUNK = 64
Q_BLOCK = 128
N_META = 16
H_GLA = 4
GLA_DK_HEAD = D_MODEL // 2 // H_GLA
GLA_DV_HEAD = D_MODEL // H_GLA
GLA_RANK = 16
GLA_TAU = 16.0
H_SB = 16
SB_HEAD_DIM = D_MODEL // H_SB
D_FF = 4 * D_MODEL
EPS = 1e-5

GLA_QK = H_GLA * GLA_DK_HEAD
GLA_V = H_GLA * GLA_DV_HEAD
SB_W = H_SB * SB_HEAD_DIM
OFF_GK = GLA_QK
OFF_GV = 2 * GLA_QK
OFF_ALOW = 2 * GLA_QK + GLA_V
OFF_SQ = OFF_ALOW + GLA_RANK
OFF_SK = OFF_SQ + SB_W
OFF_SV = OFF_SK + SB_W
OFF_GATE_A = OFF_SV + SB_W
OFF_GATE_B = OFF_GATE_A + GLA_V
D_IN = OFF_GATE_B + SB_W
IN_SPLITS = (OFF_GK, OFF_GV, OFF_ALOW, OFF_SQ, OFF_SK, OFF_SV, OFF_GATE_A, OFF_GATE_B)

kernel_name = 'gla_stickbreak_streaming_encoder'


def rms_norm(x, g):
    xf = x.astype(jnp.float32)
    y = xf * lax.rsqrt(jnp.mean(xf * xf, axis=-1, keepdims=True) + EPS)
    return (y * g.astype(jnp.float32)).astype(x.dtype)


def mixer_inputs(x, norm1_g, w_in, w_alpha_up, b_alpha):
    b, t, _ = x.shape
    h = rms_norm(x, norm1_g)
    p = h @ w_in
    gq, gk, gv, a_low, sq, sk, sv, gate_a, gate_b = jnp.split(p, IN_SPLITS, axis=-1)
    log_alpha = jax.nn.log_sigmoid((a_low @ w_alpha_up + b_alpha).astype(jnp.float32)) / GLA_TAU
    gla = tuple(a.reshape(b, t, H_GLA, -1) for a in (gq, gk, gv, log_alpha))
    sb = tuple(a.reshape(b, t, H_SB, SB_HEAD_DIM) for a in (sq, sk, sv))
    return gla, sb, gate_a, gate_b


def gla_chunked(q, k, v, log_alpha, s0):
    b, t, h, dk = q.shape
    dv = v.shape[-1]
    pad = (-t) % CHUNK
    n = (t + pad) // CHUNK

    def to_chunks(a):
        a = jnp.pad(a.astype(jnp.float32), ((0, 0), (0, pad), (0, 0), (0, 0)))
        return a.reshape(b, n, CHUNK, h, a.shape[-1]).transpose(1, 0, 3, 2, 4)

    qc = to_chunks(q) * (dk ** -0.5)
    kc = to_chunks(k)
    vc = to_chunks(v)
    gc = to_chunks(log_alpha)
    causal = jnp.tril(jnp.ones((CHUNK, CHUNK), dtype=bool))

    def step(s, inp):
        qi, ki, vi, gi = inp
        cum = jnp.cumsum(gi, axis=-2)
        q_dec = qi * jnp.exp(cum)
        k_dec = ki * jnp.exp(-cum)
        att = jnp.where(causal, jnp.einsum('bhtk,bhsk->bhts', q_dec, k_dec), 0.0)
        o = jnp.einsum('bhts,bhsv->bhtv', att, vi) + jnp.einsum('bhtk,bhkv->bhtv', q_dec, s)
        last = cum[:, :, -1, :]
        s_new = jnp.exp(last)[..., None] * s + jnp.einsum(
            'bhsk,bhsv->bhkv', ki * jnp.exp(last[:, :, None, :] - cum), vi)
        return s_new, o

    s_t, o = lax.scan(step, s0.astype(jnp.float32), (qc, kc, vc, gc))
    o = o.transpose(1, 0, 3, 2, 4).reshape(b, n * CHUNK, h, dv)[:, :t]
    return o, s_t


def stick_breaking(q, k, v, q_pos, k_pos):
    z = jnp.einsum('bqhd,bkhd->bhqk', q.astype(jnp.float32), k.astype(jnp.float32)) * (q.shape[-1] ** -0.5)
    mask = k_pos[None, :] < q_pos[:, None]
    log_1m = jnp.where(mask, jax.nn.log_sigmoid(-z), 0.0)
    after = lax.cumsum(log_1m, axis=3, reverse=True) - log_1m
    w = jnp.where(mask, jnp.exp(jax.nn.log_sigmoid(z) + after), 0.0)
    return jnp.einsum('bhqk,bkhd->bqhd', w, v.astype(jnp.float32))


def sb_prompt(q, k, v):
    b, t, h, d = q.shape
    pad = (-t) % Q_BLOCK
    nb = (t + pad) // Q_BLOCK
    qp, kp, vp = (jnp.pad(a, ((0, 0), (0, pad), (0, 0), (0, 0))) for a in (q, k, v))
    pos = jnp.arange(t + pad)
    qb = qp.reshape(b, nb, Q_BLOCK, h, d).swapaxes(0, 1)
    pb = pos.reshape(nb, Q_BLOCK)
    ob = lax.map(lambda a: stick_breaking(a[0], kp, vp, a[1], pos), (qb, pb))
    return ob.swapaxes(0, 1).reshape(b, t + pad, h, d)[:, :t]


def merge_and_ffn(x, o_gla, o_sb, gate_a, gate_b, gla_norm_g, w_out, norm2_g, w_up, w_down):
    b, t, _ = x.shape
    of = o_gla.astype(jnp.float32)
    of = of * lax.rsqrt(jnp.mean(of * of, axis=-1, keepdims=True) + EPS) * gla_norm_g.astype(jnp.float32)
    mix = (jax.nn.sigmoid(gate_a.astype(jnp.float32)) * of.reshape(b, t, -1)
           + jax.nn.sigmoid(gate_b.astype(jnp.float32)) * o_sb.astype(jnp.float32).reshape(b, t, -1))
    x = x + mix.astype(x.dtype) @ w_out
    h = rms_norm(x, norm2_g)
    x = x + jnp.square(jax.nn.relu(h @ w_up)) @ w_down
    return x


def setup_inputs(seed: int = 0) -> dict:
    key = jax.random.key(seed)
    ks = jax.random.split(key, 20)
    f32 = jnp.float32
    nrm = lambda k, shape, s: jax.random.normal(k, shape, f32) * s
    return {
        'x_prompt': nrm(ks[0], (BATCH, SEQ, D_MODEL), 1.0),
        'x_sample': nrm(ks[1], (DEC_BATCH, DEC_SEQ, D_MODEL), 1.0),
        'cache_sb_k': nrm(ks[2], (DEPTH, DEC_BATCH, PAST_LEN, H_SB, SB_HEAD_DIM), 1.0),
        'cache_sb_v': nrm(ks[3], (DEPTH, DEC_BATCH, PAST_LEN, H_SB, SB_HEAD_DIM), 1.0),
        'state_gla': nrm(ks[4], (DEPTH, DEC_BATCH, H_GLA, GLA_DK_HEAD, GLA_DV_HEAD), 1.0),
        'meta_tokens': nrm(ks[5], (N_META, D_MODEL), 1.0),
        'norm1_g': 1.0 + nrm(ks[6], (DEPTH, D_MODEL), 0.02),
        'w_in': nrm(ks[7], (DEPTH, D_MODEL, D_IN), D_MODEL ** -0.5),
        'w_alpha_up': nrm(ks[8], (DEPTH, GLA_RANK, GLA_QK), GLA_RANK ** -0.5),
        'b_alpha': nrm(ks[9], (DEPTH, GLA_QK), 0.1),
        'gla_norm_g': 1.0 + nrm(ks[10], (DEPTH, H_GLA, GLA_DV_HEAD), 0.02),
        'w_out': nrm(ks[11], (DEPTH, D_MODEL, D_MODEL), D_MODEL ** -0.5),
        'norm2_g': 1.0 + nrm(ks[12], (DEPTH, D_MODEL), 0.02),
        'w_up': nrm(ks[13], (DEPTH, D_MODEL, D_FF), D_MODEL ** -0.5),
        'w_down': nrm(ks[14], (DEPTH, D_FF, D_MODEL), D_FF ** -0.5),
        'norm_f_g': 1.0 + nrm(ks[15], (D_MODEL,), 0.02),
    }


def reference(x_prompt, x_sample, cache_sb_k, cache_sb_v, state_gla, meta_tokens, norm1_g, w_in,
              w_alpha_up, b_alpha, gla_norm_g, w_out, norm2_g, w_up, w_down, norm_f_g):
    b = x_prompt.shape[0]
    xp = jnp.concatenate(
        [jnp.broadcast_to(meta_tokens.astype(x_prompt.dtype)[None], (b, N_META, D_MODEL)), x_prompt], axis=1)
    xs = x_sample
    t_new = xs.shape[1]
    past = cache_sb_k.shape[2]
    gla_p, k_p, v_p, gla_s, k_s, v_s = [], [], [], [], [], []
    for l in range(DEPTH):
        (gq, gk, gv, ga), (sq, sk, sv), gate_a, gate_b = mixer_inputs(
            xp, norm1_g[l], w_in[l], w_alpha_up[l], b_alpha[l])
        s0 = jnp.zeros((b, H_GLA, GLA_DK_HEAD, GLA_DV_HEAD), jnp.float32)
        o_gla, s_t = gla_chunked(gq, gk, gv, ga, s0)
        o_sb = sb_prompt(sq, sk, sv)
        xp = merge_and_ffn(xp, o_gla, o_sb, gate_a, gate_b, gla_norm_g[l], w_out[l], norm2_g[l], w_up[l], w_down[l])
        gla_p.append(s_t.astype(x_prompt.dtype))
        k_p.append(sk)
        v_p.append(sv)
        (gq, gk, gv, ga), (sq, sk, sv), gate_a, gate_b = mixer_inputs(
            xs, norm1_g[l], w_in[l], w_alpha_up[l], b_alpha[l])
        o_gla, s_new = gla_chunked(gq, gk, gv, ga, state_gla[l])
        k_all = jnp.concatenate([cache_sb_k[l].astype(sk.dtype), sk], axis=1)
        v_all = jnp.concatenate([cache_sb_v[l].astype(sv.dtype), sv], axis=1)
        o_sb = stick_breaking(sq, k_all, v_all, past + jnp.arange(t_new), jnp.arange(past + t_new))
        xs = merge_and_ffn(xs, o_gla, o_sb, gate_a, gate_b, gla_norm_g[l], w_out[l], norm2_g[l], w_up[l], w_down[l])
        gla_s.append(s_new.astype(x_sample.dtype))
        k_s.append(sk)
        v_s.append(sv)
    y_prompt = rms_norm(xp, norm_f_g)[:, N_META:]
    y_sample = rms_norm(xs, norm_f_g)
    return (y_prompt, y_sample, jnp.stack(gla_p), jnp.stack(k_p), jnp.stack(v_p),
            jnp.stack(gla_s), jnp.stack(k_s), jnp.stack(v_s))
```

```python
import numpy as np
import concourse.bass as bass
import concourse.mybir as mybir
from concourse.bass_utils import run_bass_kernel_spmd
from contextlib import ExitStack

F32 = mybir.dt.float32
BF16 = mybir.dt.bfloat16
F32R = mybir.dt.float32r
ALU = mybir.AluOpType
AF = mybir.ActivationFunctionType

PE, ACT, DVE, POOL, SP = "tensor", "scalar", "vector", "gpsimd", "sync"
ENGS = (PE, ACT, DVE, POOL, SP)

D = 2048
NCH = 16
H_SB = 16
H_GLA = 4
DK = 256
DV = 512
DFF = 8192
N_META = 16
EPS = 1e-5
TS = 32
NS = 2
D_IN = 14352
OFF_GK, OFF_GV, OFF_ALOW, OFF_SQ, OFF_SK, OFF_SV, OFF_GA, OFF_GB = 1024, 2048, 4096, 4112, 6160, 8208, 10256, 12304
SBUF_BASE = 16512
SBUF_TOP = 229376


class Op:
    __slots__ = ("eng", "emit", "waits", "signal", "semval", "dma_grp", "dma_val")

    def __init__(self, eng, emit, dma_grp=None):
        self.eng = eng
        self.emit = emit
        self.waits = []
        self.signal = False
        self.semval = None
        self.dma_grp = dma_grp
        self.dma_val = None


class Prog:
    def __init__(self, nc):
        self.nc = nc
        self.ops = {e: [] for e in ENGS}
        self.last_write = {}
        self.readers = {}
        self.dma_count = {}
        self.dma_cons = {}
        self.bar = None
        self.bar_pending = set()

    def barrier(self, skip_dma=()):
        b = []
        for e in ENGS:
            for o in reversed(self.ops[e]):
                if o.dma_grp is None:
                    o.signal = True
                    b.append(("eng", o))
                    break
        for g, c in self.dma_count.items():
            if g not in skip_dma:
                b.append(("dma", g, 16 * c))
        self.bar = b
        self.bar_pending = set(ENGS)

    def op(self, eng, emit, reads=(), writes=(), dma=None, nobar=False):
        o = Op(eng, emit, dma_grp=dma)
        deps = []
        for r in reads:
            lw = self.last_write.get(r)
            if lw is not None:
                deps.append((lw, "raw"))
        for w in writes:
            lw = self.last_write.get(w)
            if lw is not None:
                deps.append((lw, "waw"))
            for rd in self.readers.get(w, ()):
                deps.append((rd, "war"))
        seen = set()
        for d, kind in deps:
            if d is o:
                continue
            if d.dma_grp is not None:
                key = ("dma", d.dma_grp)
                if key in seen:
                    continue
                seen.add(key)
                o.waits.append(("dma", d.dma_grp, 16 * self.dma_count[d.dma_grp]))
                self.dma_cons.setdefault(d.dma_grp, []).append(o)
                continue
            if d.eng == eng and dma is None:
                if eng == PE:
                    continue
            if id(d) in seen:
                continue
            seen.add(id(d))
            d.signal = True
            o.waits.append(("eng", d))
        if eng in self.bar_pending and not nobar:
            self.bar_pending.discard(eng)
            for w in self.bar:
                if w[0] == "eng" and w[1].eng == eng and dma is None:
                    continue
                o.waits.append(w)
        for r in reads:
            self.readers.setdefault(r, []).append(o)
        for w in writes:
            self.last_write[w] = o
            self.readers[w] = []
        if dma is not None:
            for c in self.dma_cons.pop(dma, ()):
                if c is o or c.eng == eng:
                    continue
                if c.dma_grp is not None:
                    o.waits.append(("dma", c.dma_grp, c.dma_val))
                else:
                    c.signal = True
                    o.waits.append(("eng", c))
            self.dma_count[dma] = self.dma_count.get(dma, 0) + 1
            o.dma_val = 16 * self.dma_count[dma]
        self.ops[eng].append(o)
        return o

    def emit(self):
        nc = self.nc
        with ExitStack() as es:
            esem = {e: es.enter_context(nc.semaphore("sem_" + e)) for e in (PE, ACT, DVE, POOL)}
            dsem = {g: es.enter_context(nc.semaphore("dsem_" + str(g))) for g in self.dma_count}
            for e in ENGS:
                c = 0
                for o in self.ops[e]:
                    if o.dma_grp is None and o.signal:
                        c += 1
                        o.semval = c
            block = es.enter_context(nc.Block())

            def run(engname):
                def body(eng):
                    waited = {}
                    for o in self.ops[engname]:
                        for w in o.waits:
                            if w[0] == "dma":
                                sem, val, key = dsem[w[1]], w[2], ("d", w[1])
                            else:
                                d = w[1]
                                sem, val, key = esem[d.eng], d.semval, ("e", d.eng)
                            if waited.get(key, 0) >= val:
                                continue
                            waited[key] = val
                            eng.wait_ge(sem, val)
                        ins = o.emit(eng)
                        if o.dma_grp is not None:
                            ins.then_inc(dsem[o.dma_grp], 16)
                        elif o.signal:
                            ins.then_inc(esem[engname], 1)
                    if engname == SP:
                        for g, c in self.dma_count.items():
                            eng.wait_ge(dsem[g], 16 * c)
                return body

            block.tensor(run(PE))
            block.scalar(run(ACT))
            block.vector(run(DVE))
            block.gpsimd(run(POOL))
            block.sync(run(SP))


class Alloc:
    def __init__(self, nc, base, top):
        self.nc, self.off, self.top = nc, base, top
        self.n = 0

    def mark(self):
        return self.off

    def reset(self, off):
        self.off = off

    def __call__(self, shape, dt):
        per = 1
        for s in shape[1:]:
            per *= s
        nbytes = per * (4 if dt in (F32, F32R) else 2)
        nbytes = (nbytes + 63) // 64 * 64
        assert self.off + nbytes <= self.top, ("SBUF overflow", self.off, nbytes, self.top)
        self.n += 1
        t = self.nc.alloc_sbuf_tensor_at("t%d" % self.n, list(shape), dt, offset=self.off)
        self.off += nbytes
        return t


def build_nc(SEQ, PAST):
    assert SEQ % 512 == 0 and PAST % 128 == 0
    NG_ALL = SEQ // 512
    NSP = 4 if NG_ALL % 4 == 0 else 1
    OWN_G = NG_ALL // NSP
    NGP = NG_ALL - OWN_G
    OWN = OWN_G * 512
    IDX = 128 + SEQ
    T_P = IDX
    TKP = IDX
    NKT_P = TKP // 128
    NPT = PAST // 128
    NTOK_S = NS * TS

    nc = bass.Bass("TRN2", target_bir_lowering=False)
    dr = lambda name, shape, dt=F32, kind="ExternalInput": nc.dram_tensor(name, list(shape), dt, kind=kind)
    xp = dr("xp", [T_P, D])
    xsm = dr("xs", [NTOK_S, D])
    ck = dr("ck", [NS * PAST, D])
    cv = dr("cv", [NS * PAST, D])
    st = dr("st", [NS * H_GLA * DK, DV])
    w_in = dr("w_in", [D, D_IN])
    w_out = dr("w_out", [D, D])
    w_up = dr("w_up", [D, DFF])
    w_down = dr("w_down", [DFF, D])
    w_al = dr("w_al", [16, 1024])
    b_al = dr("b_al", [1, 1024])
    gvec = dr("gvec", [4 * 16, 128])
    gf_row = dr("gf_row", [1, D])
    o_yp = dr("o_yp", [OWN, D], kind="ExternalOutput")
    o_ys = dr("o_ys", [NTOK_S, D], kind="ExternalOutput")
    o_gp = dr("o_gp", [H_GLA * DK, DV], kind="ExternalOutput")
    o_kp = dr("o_kp", [N_META + OWN, D], kind="ExternalOutput")
    o_vp = dr("o_vp", [N_META + OWN, D], kind="ExternalOutput")
    o_gs = dr("o_gs", [NS * H_GLA * DK, DV], kind="ExternalOutput")
    o_ks = dr("o_ks", [NTOK_S, D], kind="ExternalOutput")
    o_vs = dr("o_vs", [NTOK_S, D], kind="ExternalOutput")
    TKS = TKP + 128
    kts = nc.dram_tensor("kts", [H_SB * 128, TKS], BF16)
    vts = nc.dram_tensor("vts", [H_SB * TKS, 128], BF16)

    P = Prog(nc)
    A = Alloc(nc, SBUF_BASE, SBUF_TOP)
    es = ExitStack()
    psb = [es.enter_context(nc.psum_tensor("ps%d" % i, [128, 512], F32)) for i in range(8)]

    ident = A([128, 128], F32)
    ones32 = A([128, 128], F32)
    ones16 = A([128, 128], BF16)
    ident16 = A([128, 128], BF16)
    negT = A([128, 128], BF16)
    negTr = A([128, 128], F32R)
    onesr = A([128, 128], F32R)
    nonesr = A([128, 128], F32R)
    U32 = A([128, 64], F32)
    LsHi = A([128, 128], F32)
    Ls32 = A([64, 64], F32)
    masks = A([128, 4, 512], BF16)
    gT = A([128, 64], F32)
    gtmp = A([64, 128], F32)
    walp = A([17, 1024], F32)
    S = A([128, H_GLA, 2, DV], F32)
    Sb = A([128, H_GLA, 2, DV], BF16)
    xin = A([128, D], F32)
    xT = A([128, NCH, 512], BF16)
    rstd_bc = A([128, 512], F32)
    rstd_tm = A([128, 8], F32)
    ss = A([128, 8], F32)
    rrow = A([1, 128], F32)
    wb = [A([128, NCH, 512], BF16) for _ in range(2)]
    mixT = A([128, NCH, 512], BF16)
    junk = mixT[:].rearrange("p c n -> p (c n)")
    base_mark = A.mark()

    cnt = {"wb": 0, "ps": 0}
    WBG2 = ("wb0", "wb1")
    WBG4 = ("wb0", "wb1", "wb2", "wb3")
    wbs = list(wb)

    def ps_gen():
        i = cnt["ps"] % 2
        cnt["ps"] += 1
        return psb[i], ("ps", i)

    P.op(POOL, lambda e: e.memset(ident[:], 0.0), writes=["ident"])
    P.op(POOL, lambda e: e.affine_select(out=ident[:], in_=ident[:], pattern=[[-1, 128]], base=0, channel_multiplier=1,
                                          compare_op=ALU.not_equal, fill=1.0), reads=["ident"], writes=["ident"])
    P.op(POOL, lambda e: e.memset(ones32[:], 1.0), writes=["ones32"])
    P.op(POOL, lambda e: e.memset(ones16[:], 1.0), writes=["ones16"])
    P.op(POOL, lambda e: e.memset(negT[:], -1.0), writes=["negT"])
    P.op(POOL, lambda e: e.affine_select(out=negT[:], in_=negT[:], pattern=[[-1, 128]], base=0, channel_multiplier=1,
                                          compare_op=ALU.is_ge, fill=0.0), reads=["negT"], writes=["negT"])
    P.op(DVE, lambda e: e.tensor_copy(out=negTr[:], in_=negT[:]), reads=["negT"], writes=["negTr"])
    P.op(DVE, lambda e: e.tensor_copy(out=onesr[:], in_=ones32[:]), reads=["ones32"], writes=["onesr"])
    P.op(DVE, lambda e: e.tensor_scalar(out=nonesr[:], in0=ones32[:], scalar1=-1.0, scalar2=None, op0=ALU.mult),
         reads=["ones32"], writes=["nonesr"])
    P.op(POOL, lambda e: e.memset(U32[0:64, :], 1.0), writes=["U32"])
    P.op(POOL, lambda e: e.affine_select(out=U32[0:64, :], in_=U32[0:64, :], pattern=[[1, 64]], base=0, channel_multiplier=-1,
                                          compare_op=ALU.is_ge, fill=0.0), reads=["U32"], writes=["U32"])
    P.op(SP, lambda e: e.dma_start(out=U32[64:128, :], in_=U32[0:64, :]), reads=["U32"], writes=["U32hi"], dma="c3")
    P.op(POOL, lambda e: e.memset(LsHi[:], 0.0), writes=["LsHi"])
    P.op(POOL, lambda e: e.memset(Ls32[:], 1.0), writes=["Ls32"])
    P.op(POOL, lambda e: e.affine_select(out=Ls32[:], in_=Ls32[:], pattern=[[-1, 64]], base=0, channel_multiplier=1,
                                          compare_op=ALU.is_gt, fill=0.0), reads=["Ls32"], writes=["Ls32"])
    P.op(SP, lambda e: e.dma_start(out=LsHi[64:128, 64:128], in_=Ls32[0:64, 0:64]), reads=["Ls32", "LsHi"], writes=["LsHi"], dma="c4")
    P.op(POOL, lambda e: e.memset(masks[:], 1.0), writes=["masks"])
    for i in range(4):
        P.op(POOL, lambda e, i=i: e.affine_select(out=masks[:, i, :], in_=masks[:, i, :], pattern=[[1, 512]],
                                                  base=-128 * i, channel_multiplier=-1, compare_op=ALU.is_gt, fill=0.0),
             reads=["masks"], writes=["masks"])
    P.op(DVE, lambda e: e.tensor_scalar(out=masks[:], in0=masks[:], scalar1=-1.0, scalar2=29952.0, op0=ALU.add, op1=ALU.mult),
         reads=["masks"], writes=["masks"])
    P.op(DVE, lambda e: e.tensor_copy(out=ident16[:], in_=ident[:]), reads=["ident"], writes=["ident16"])
    P.op(SP, lambda e: e.dma_start(out=gtmp[:], in_=gvec.ap()), writes=["gtmp"], dma="c0")
    P.op(PE, lambda e: e.transpose(out=psb[7][:, 0:64], in_=gtmp[:], identity=ident[0:64, 0:64]),
         reads=["gtmp", "ident"], writes=[("ps", 7)])
    P.op(DVE, lambda e: e.tensor_copy(out=gT[:], in_=psb[7][:, 0:64]), reads=[("ps", 7)], writes=["gT"])
    P.op(SP, lambda e: e.dma_start(out=walp[0:16, :], in_=w_al.ap()), writes=["walpw"], dma="c1")
    P.op(SP, lambda e: e.dma_start(out=walp[16:17, :], in_=b_al.ap()), writes=["walpb"], dma="c2")
    P.op(DVE, lambda e: e.memset(S[:], 0.0), writes=[("S", h) for h in range(4)])
    P.op(DVE, lambda e: e.memset(Sb[:], 0.0), writes=[("Sb", h) for h in range(4)])

    w_in_v = w_in.ap().rearrange("(c p) m -> p c m", p=128)
    w_out_v = w_out.ap().rearrange("(c p) m -> p c m", p=128)
    w_up_v = w_up.ap().rearrange("(c p) m -> p c m", p=128)

    def load_w(view, c0, m, part=0):
        i = cnt["wb"] % len(wbs)
        cnt["wb"] += 1
        buf = wbs[i]
        P.op(POOL, lambda e: e.dma_start(out=buf[:, :, 0:m], in_=view[:, :, c0:c0 + m]),
             writes=[("wb", i)], dma="wb%d" % i, nobar=(i < 2))
        return buf, ("wb", i)

    def norm_T(tiles, N, gcol, src_is_acc=None):
        for ti, (src, n, col0) in enumerate(tiles):
            if src_is_acc is None:
                P.op(SP, lambda e, src=src, n=n: e.dma_start(out=xin[0:n, :], in_=src), writes=["xin"], dma="xin")
                xs_ap = xin[0:n, :]
                xkey = "xin"
            else:
                xs_ap = src_is_acc[0:n, ti, :]
                xkey = ("xacc", ti)
            P.op(ACT, lambda e, xs_ap=xs_ap, n=n, ti=ti: e.activation(out=junk[0:n, 0:D], in_=xs_ap, func=AF.Square,
                                                                      accum_out=ss[0:n, ti:ti + 1]),
                 reads=[xkey], writes=["mixT", ("ss", ti)])
            P.op(DVE, lambda e, n=n, ti=ti: e.tensor_scalar(out=rstd_tm[0:n, ti:ti + 1], in0=ss[0:n, ti:ti + 1],
                                                            scalar1=1.0 / D, scalar2=EPS, op0=ALU.mult, op1=ALU.add),
                 reads=[("ss", ti)], writes=[("rstd", ti)])
            P.op(ACT, lambda e, n=n, ti=ti: e.activation(out=rstd_tm[0:n, ti:ti + 1], in_=rstd_tm[0:n, ti:ti + 1],
                                                         func=AF.Sqrt), reads=[("rstd", ti)], writes=[("rstd", ti)])
            P.op(DVE, lambda e, n=n, ti=ti: e.reciprocal(out=rstd_tm[0:n, ti:ti + 1], in_=rstd_tm[0:n, ti:ti + 1]),
                 reads=[("rstd", ti)], writes=[("rstd", ti)])
            for q4 in range(4):
                bank = 4 + (q4 % 2)
                for cc in range(4):
                    c = q4 * 4 + cc
                    P.op(PE, lambda e, bank=bank, cc=cc, c=c, n=n, xs_ap=xs_ap: e.transpose(
                        out=psb[bank][:, cc * 128:cc * 128 + n], in_=xs_ap[:, c * 128:(c + 1) * 128],
                        identity=ident[0:n, 0:n]), reads=[xkey, "ident"], writes=[("ps", bank)])
                for cc in range(4):
                    c = q4 * 4 + cc
                    eng = ACT if cc % 2 == 0 else DVE
                    if eng == ACT:
                        P.op(ACT, lambda e, bank=bank, cc=cc, c=c, n=n, col0=col0: e.activation(
                            out=xT[:, c, col0:col0 + n], in_=psb[bank][:, cc * 128:cc * 128 + n], func=AF.Copy,
                            scale=gT[:, gcol + c:gcol + c + 1]), reads=[("ps", bank), "gT"], writes=["xT"])
                    else:
                        P.op(DVE, lambda e, bank=bank, cc=cc, c=c, n=n, col0=col0: e.tensor_scalar(
                            out=xT[:, c, col0:col0 + n], in0=psb[bank][:, cc * 128:cc * 128 + n],
                            scalar1=gT[:, gcol + c:gcol + c + 1], scalar2=None, op0=ALU.mult),
                            reads=[("ps", bank), "gT"], writes=["xT"])
            P.op(PE, lambda e, n=n, ti=ti: e.transpose(out=psb[6][0:1, 0:n], in_=rstd_tm[0:n, ti:ti + 1],
                                                       identity=ident[0:n, 0:n]),
                 reads=[("rstd", ti), "ident"], writes=[("ps", 6)])
            P.op(DVE, lambda e, n=n: e.tensor_copy(out=rrow[0:1, 0:n], in_=psb[6][0:1, 0:n]),
                 reads=[("ps", 6)], writes=["rrow"])
            P.op(PE, lambda e, n=n: e.matmul(psb[6][:, 128:128 + n], lhsT=ones32[0:1, :], rhs=rrow[0:1, 0:n],
                                             start=True, stop=True), reads=["rrow", "ones32"], writes=[("ps", 6)])
            P.op(DVE, lambda e, n=n, col0=col0: e.tensor_copy(out=rstd_bc[:, col0:col0 + n], in_=psb[6][:, 128:128 + n]),
                 reads=[("ps", 6)], writes=["rstd_bc"])

    def proj_fm(wbuf, wkey, wc0, m, N, evac):
        pt, pkey = ps_gen()
        for c in range(NCH):
            P.op(PE, lambda e, c=c, pt=pt: e.matmul(pt[0:m, 0:N], lhsT=wbuf[:, c, wc0:wc0 + m], rhs=xT[:, c, 0:N],
                                                    start=(c == 0), stop=(c == NCH - 1)),
                 reads=[wkey, "xT"], writes=[pkey])
        evac(pt, pkey)

    def proj_tm(wbuf, wkey, wc0, m, col0, n, evac):
        pt, pkey = ps_gen()
        for c in range(NCH):
            P.op(PE, lambda e, c=c, pt=pt: e.matmul(pt[0:n, 0:m], lhsT=xT[:, c, col0:col0 + n],
                                                    rhs=wbuf[:, c, wc0:wc0 + m], start=(c == 0), stop=(c == NCH - 1)),
                 reads=[wkey, "xT"], writes=[pkey])
        evac(pt, pkey)

    def process_group(gi, tiles, N, gla_segs, sb_segs, out_rows, lite=False):
        A.reset(base_mark)
        ntile = len(tiles)
        QT4 = A([128, 4, 512], BF16)
        KT4 = A([128, 4, 512], BF16)
        m_vt = A.mark()
        Vt = A([128, 4, 512], BF16)
        m_st = A.mark()
        stage = [A([128, 512], F32) for _ in range(2)]
        gbS = A([128, 4, 512], BF16)
        tmpB = A([128, 512], F32)
        gaS = A([128, 4, 512], BF16)
        A32 = A([128, 4, 512], F32)
        alow = A([17, 512], F32)

        klen = max([(PAST + TS) if sg_[0] == "s" else (sg_[4] + sg_[3]) for sg_ in sb_segs])
        klen = ((klen + 127) // 128) * 128
        del wbs[2:]

        def sb_bufs():
            if lite:
                return None
            return {"kt": A([128, klen], BF16), "vt": A([128, klen // 128, 128], BF16), "e32": A([128, 512], F32R),
                    "e32b": A([128, 512], F32R), "w16": A([128, 512], BF16),
                    "carry": A([128, 512], F32R), "kc": A([128, 4, 128], F32), "tB": A([128, 512], F32)}
        SBB = [sb_bufs()]
        _save = A.mark()
        A.reset(m_st)
        if not lite:
            SBB[0]["arg"] = A([128, 512], F32)
        A.reset(m_vt)
        _x_arg = A([128, 512], F32)
        A.reset(_save)
        m_gla = A.mark()
        qT = A([128, 2, 512], BF16)
        kTg = A([128, 2, 512], BF16)
        k_tm = A([64, 8, 256], BF16)
        v_tm = A([64, 8, 512], BF16)
        g_tm = A([64, 8, 256], F32)
        sq16 = A([128, 4, 512], BF16)
        ecum = A([128, 2, 64], F32)
        encum = A([128, 2, 64], F32)
        qd = A([128, 2, 64], BF16)
        kd = A([128, 2, 64], BF16)
        erev = A([64, 256], F32)
        kd2 = A([64, 256], BF16)
        attT = A([64, 64], BF16)
        m_end = A.mark()
        A.reset(m_gla)
        SBB.append(sb_bufs())
        if not lite:
            SBB[1]["arg"] = _x_arg
        A.reset(max(m_end, A.mark()))
        if lite:
            k_tm2 = A([128, 4, 256], BF16)
            v_tm2 = A([128, 4, 512], BF16)
            g_tm2 = A([128, 4, 256], F32)
            erev2 = A([128, 256], F32)
            kd2b = A([128, 256], BF16)
            elast = A([128, 2, 1], F32)
        while A.top - A.mark() >= 16384 + 64 and len(wbs) < 4:
            wbs.append(A([128, NCH, 512], BF16))
        P.barrier(skip_dma=WBG2)

        norm_T(tiles, N, 0)

        P.op(DVE, lambda e: e.memset(alow[:], 1.0), writes=["alow"])
        wbuf, wkey = load_w(w_in_v, OFF_ALOW, 16)
        proj_fm(wbuf, wkey, 0, 16, N, lambda pt, pkey: P.op(
            DVE, lambda e: e.tensor_tensor(out=alow[0:16, 0:N], in0=pt[0:16, 0:N], in1=rstd_bc[0:16, 0:N], op=ALU.mult),
            reads=[pkey, "rstd_bc"], writes=["alow"]))

        all_chunks = []
        for (kind, seq, chunks) in gla_segs:
            for ch in chunks:
                all_chunks.append(ch)
        def do_quad_sb_lite(hg):
            h0 = 4 * hg
            wbuf, wkey = load_w(w_in_v, OFF_SK + 512 * hg, 512)
            for h4 in range(4):
                proj_fm(wbuf, wkey, h4 * 128, 128, N, lambda pt, pkey, h4=h4: P.op(
                    DVE, lambda e: e.tensor_tensor(out=KT4[:, h4, 0:N], in0=pt[:, 0:N], in1=rstd_bc[:, 0:N], op=ALU.mult),
                    reads=[pkey, "rstd_bc"], writes=["KT4"]))
                yield
            for (kind, seq, c0, NQ, pos0) in sb_segs:
                for h4 in range(4):
                    h = h0 + h4
                    P.op(SP, lambda e, h=h, h4=h4, c0=c0, NQ=NQ, pos0=pos0: e.dma_start(
                        out=kts.ap()[h * 128:(h + 1) * 128, pos0:pos0 + NQ], in_=KT4[:, h4, c0:c0 + NQ]),
                        reads=["KT4"], writes=[("kts", h)], dma="kts")
            def kv_rows(pt, pkey, ti, n, spec, is_v):
                dst, r0, p0, cntr = spec
                sg = stage[1 if is_v else 0]
                sk_ = ("stage", 1 if is_v else 0)
                P.op(ACT, lambda e: e.activation(out=sg[0:n, :], in_=pt[0:n, 0:512], func=AF.Copy,
                                                 scale=rstd_tm[0:n, ti:ti + 1]),
                     reads=[pkey, ("rstd", ti)], writes=[sk_])
                P.op(SP, lambda e: e.dma_start(out=dst.ap()[r0:r0 + cntr, 512 * hg:512 * hg + 512], in_=sg[p0:p0 + cntr, :]),
                     reads=[sk_], dma="stage%d" % (1 if is_v else 0))
                if is_v:
                    P.op(DVE, lambda e: e.tensor_copy(out=Vt[0:n, ti, :], in_=sg[0:n, :]), reads=[sk_], writes=[("Vt", ti)])

            for ti, (src, n, col0) in enumerate(tiles):
                spec = out_rows[ti][1]
                if spec[0] is not None:
                    proj_tm(wbuf, wkey, 0, 512, col0, n, lambda pt, pkey, ti=ti, n=n, spec=spec: kv_rows(
                        pt, pkey, ti, n, spec, False))
                    yield
            wbuf, wkey = load_w(w_in_v, OFF_SV + 512 * hg, 512)
            for ti, (src, n, col0) in enumerate(tiles):
                srow = out_rows[ti][3]
                spec = out_rows[ti][2]

                def evl(pt, pkey, ti=ti, n=n):
                    P.op(ACT, lambda e: e.activation(out=Vt[0:n, ti, :], in_=pt[0:n, 0:512], func=AF.Copy,
                                                     scale=rstd_tm[0:n, ti:ti + 1]),
                         reads=[pkey, ("rstd", ti)], writes=[("Vt", ti)])
                if spec[0] is not None:
                    proj_tm(wbuf, wkey, 0, 512, col0, n, lambda pt, pkey, ti=ti, n=n, spec=spec: kv_rows(
                        pt, pkey, ti, n, spec, True))
                    yield
                else:
                    proj_tm(wbuf, wkey, 0, 512, col0, n, evl)
                    yield
                for h4 in range(4):
                    h = h0 + h4
                    P.op(SP, lambda e, h=h, h4=h4, ti=ti, n=n, srow=srow: e.dma_start(
                        out=vts.ap()[h * TKS + srow:h * TKS + srow + n, :], in_=Vt[0:n, ti, h4 * 128:(h4 + 1) * 128]),
                        reads=[("Vt", ti)], writes=[("vts", h)], dma="vts")

        def do_quad_lite(hg):
            wbuf, wkey = load_w(w_in_v, OFF_GK + 256 * hg, 256)
            for ti, (src, n, col0) in enumerate(tiles):
                proj_tm(wbuf, wkey, 0, 256, col0, n, lambda pt, pkey, ti=ti, n=n: P.op(
                    ACT, lambda e: e.activation(out=k_tm2[0:n, ti, :], in_=pt[0:n, 0:256], func=AF.Copy,
                                                scale=rstd_tm[0:n, ti:ti + 1]),
                    reads=[pkey, ("rstd", ti)], writes=[("k_tm2", ti)]))
            wbuf, wkey = load_w(w_in_v, OFF_GV + 512 * hg, 512)
            for ti, (src, n, col0) in enumerate(tiles):
                proj_tm(wbuf, wkey, 0, 512, col0, n, lambda pt, pkey, ti=ti, n=n: P.op(
                    ACT, lambda e: e.activation(out=v_tm2[0:n, ti, :], in_=pt[0:n, 0:512], func=AF.Copy,
                                                scale=rstd_tm[0:n, ti:ti + 1]),
                    reads=[pkey, ("rstd", ti)], writes=[("v_tm2", ti)]))
            for ti, (src, n, col0) in enumerate(tiles):
                P.op(PE, lambda e, col0=col0, n=n: e.matmul(psb[6][0:n, 0:256], lhsT=alow[0:17, col0:col0 + n],
                                                            rhs=walp[0:17, 256 * hg:256 * hg + 256], start=True, stop=True),
                     reads=["alow", "walpw", "walpb"], writes=[("ps", 6)])
                P.op(ACT, lambda e, n=n: e.activation(out=erev2[0:n, :], in_=psb[6][0:n, 0:256], func=AF.Exp, scale=-1.0),
                     reads=[("ps", 6)], writes=["erev2"])
                P.op(ACT, lambda e, n=n: e.activation(out=erev2[0:n, :], in_=erev2[0:n, :], func=AF.Ln, bias=1.0),
                     reads=["erev2"], writes=["erev2"])
                P.op(DVE, lambda e, n=n, ti=ti: e.tensor_scalar(out=g_tm2[0:n, ti, :], in0=erev2[0:n, :],
                                                                scalar1=-1.0 / 16.0, scalar2=None, op0=ALU.mult),
                     reads=["erev2"], writes=[("g_tm2", ti)])

            def gen_chunks_lite():
                nch = len(all_chunks)
                for ci, (cc, ncn) in enumerate(all_chunks):
                    assert ncn == 64
                    ti, p0 = cc // 128, cc % 128
                    gch = g_tm2[p0:p0 + 64, ti, :]
                    ukey = "U32" if p0 == 0 else "U32hi"
                    for kc in range(2):
                        P.op(PE, lambda e, kc=kc, gch=gch, p0=p0: e.matmul(
                            psb[2][:, kc * 64:kc * 64 + 64], lhsT=gch[:, kc * 128:(kc + 1) * 128], rhs=U32[p0:p0 + 64, 0:64],
                            start=True, stop=True), reads=[("g_tm2", ti), ukey], writes=[("ps", 2)])
                    if p0 == 0:
                        P.op(PE, lambda e, gch=gch: e.matmul(psb[3][0:64, 0:256], lhsT=Ls32[0:64, 0:64], rhs=gch,
                                                            start=True, stop=True),
                             reads=[("g_tm2", ti), "Ls32"], writes=[("ps", 3)])
                    else:
                        P.op(PE, lambda e, gch=gch: e.matmul(psb[3][:, 0:256], lhsT=LsHi[64:128, :], rhs=gch,
                                                            start=True, stop=True),
                             reads=[("g_tm2", ti), "LsHi"], writes=[("ps", 3)])
                    plast = psb[2][:, 0:128].rearrange("p (k t) -> p k t", k=2)[:, :, 63:64]
                    P.op(ACT, lambda e, plast=plast: e.activation(out=elast[:, :, :], in_=plast, func=AF.Exp),
                         reads=[("ps", 2)], writes=["elast"])
                    P.op(ACT, lambda e, p0=p0: e.activation(out=erev2[p0:p0 + 64, :], in_=psb[3][p0:p0 + 64, 0:256], func=AF.Exp),
                         reads=[("ps", 3)], writes=["erev2"])
                    P.op(DVE, lambda e, p0=p0, ti=ti: e.tensor_tensor(out=kd2b[p0:p0 + 64, :], in0=k_tm2[p0:p0 + 64, ti, :],
                                                                      in1=erev2[p0:p0 + 64, :], op=ALU.mult),
                         reads=[("k_tm2", ti), "erev2"], writes=["kd2b"])
                    yield
                    for kc in range(2):
                        P.op(PE, lambda e, kc=kc, p0=p0, ti=ti: e.matmul(
                            psb[7][:, 0:512], lhsT=kd2b[p0:p0 + 64, kc * 128:(kc + 1) * 128],
                            rhs=v_tm2[p0:p0 + 64, ti, :], start=True, stop=True),
                            reads=["kd2b", ("v_tm2", ti)], writes=[("ps", 7)])
                        P.op(DVE, lambda e, kc=kc: e.scalar_tensor_tensor(
                            out=S[:, hg, kc, :], in0=S[:, hg, kc, :], scalar=elast[:, kc, 0:1], in1=psb[7][:, 0:512],
                            op0=ALU.mult, op1=ALU.add), reads=[("S", hg), "elast", ("ps", 7)], writes=[("S", hg)])
                    if ci == nch - 1:
                        P.op(ACT, lambda e: e.activation(out=Sb[:, hg, :, :], in_=S[:, hg, :, :], func=AF.Copy),
                             reads=[("S", hg)], writes=[("Sb", hg)])
                    yield

            _gens = [gen_chunks_lite(), do_quad_sb_lite(hg)]
            while _gens:
                for _g in list(_gens):
                    try:
                        next(_g)
                    except StopIteration:
                        _gens.remove(_g)

        def do_quad(hg):
            if not lite:
                wbuf, wkey = load_w(w_in_v, 256 * hg, 256)
            for kc in range(0 if lite else 2):
                proj_fm(wbuf, wkey, kc * 128, 128, N, lambda pt, pkey, kc=kc: P.op(
                    DVE, lambda e: e.scalar_tensor_tensor(out=qT[:, kc, 0:N], in0=pt[:, 0:N], scalar=DK ** -0.5,
                                                          in1=rstd_bc[:, 0:N], op0=ALU.mult, op1=ALU.mult),
                    reads=[pkey, "rstd_bc"], writes=["qT"]))
            wbuf, wkey = load_w(w_in_v, OFF_GK + 256 * hg, 256)
            for kc in range(0 if lite else 2):
                proj_fm(wbuf, wkey, kc * 128, 128, N, lambda pt, pkey, kc=kc: P.op(
                    DVE, lambda e: e.tensor_tensor(out=kTg[:, kc, 0:N], in0=pt[:, 0:N], in1=rstd_bc[:, 0:N], op=ALU.mult),
                    reads=[pkey, "rstd_bc"], writes=["kTg"]))
            rt_of = {}
            for ti, (src, n, col0) in enumerate(tiles):
                for ci, (cc, ncn) in enumerate(all_chunks):
                    if col0 <= cc < col0 + n:
                        rt_of[ci] = (ti, cc - col0)
            for ci, (cc, ncn) in enumerate(all_chunks):
                ti, p0 = rt_of[ci]
                pass
            for ci, (cc, ncn) in enumerate(all_chunks):
                if hg == 0:
                    P.op(PE, lambda e, cc=cc, ncn=ncn: e.transpose(out=psb[6][0:ncn, 256:257], in_=rstd_bc[0:1, cc:cc + ncn],
                                                                  identity=ident[0:1, 0:1]),
                         reads=["rstd_bc", "ident"], writes=[("ps", 6)])
                    P.op(DVE, lambda e, ci=ci, ncn=ncn: e.tensor_copy(out=ss[0:ncn, 0:1] if False else rs_tok[0:ncn, ci:ci + 1],
                                                                      in_=psb[6][0:ncn, 256:257]),
                         reads=[("ps", 6)], writes=[("rs_tok", ci)])
            for ci, (cc, ncn) in enumerate(all_chunks):
                proj_tm(wbuf, wkey, 0, 256, cc, ncn, lambda pt, pkey, ci=ci, ncn=ncn: P.op(
                    ACT, lambda e: e.activation(out=k_tm[0:ncn, ci, :], in_=pt[0:ncn, 0:256], func=AF.Copy,
                                                scale=rs_tok[0:ncn, ci:ci + 1]),
                    reads=[pkey, ("rs_tok", ci)], writes=[("k_tm", ci)]))
            wbuf, wkey = load_w(w_in_v, OFF_GV + 512 * hg, 512)
            for ci, (cc, ncn) in enumerate(all_chunks):
                proj_tm(wbuf, wkey, 0, 512, cc, ncn, lambda pt, pkey, ci=ci, ncn=ncn: P.op(
                    ACT, lambda e: e.activation(out=v_tm[0:ncn, ci, :], in_=pt[0:ncn, 0:512], func=AF.Copy,
                                                scale=rs_tok[0:ncn, ci:ci + 1]),
                    reads=[pkey, ("rs_tok", ci)], writes=[("v_tm", ci)]))
            if not lite:
                wbuf, wkey = load_w(w_in_v, OFF_GA + 512 * hg, 512)
            for dc in range(0 if lite else 4):
                def ev(pt, pkey, dc=dc):
                    P.op(DVE, lambda e: e.tensor_tensor(out=tmpB[:, 0:N], in0=pt[:, 0:N], in1=rstd_bc[:, 0:N], op=ALU.mult),
                         reads=[pkey, "rstd_bc"], writes=["tmpB"])
                    P.op(ACT, lambda e: e.activation(out=gaS[:, dc, 0:N], in_=tmpB[:, 0:N], func=AF.Sigmoid),
                         reads=["tmpB"], writes=["gaS"])
                proj_fm(wbuf, wkey, dc * 128, 128, N, ev)
            for ci, (cc, ncn) in enumerate(all_chunks):
                P.op(PE, lambda e, cc=cc, ncn=ncn: e.matmul(psb[6][0:ncn, 0:256], lhsT=alow[0:17, cc:cc + ncn],
                                                            rhs=walp[0:17, 256 * hg:256 * hg + 256], start=True, stop=True),
                     reads=["alow", "walpw", "walpb"], writes=[("ps", 6)])
                P.op(ACT, lambda e, ncn=ncn: e.activation(out=erev[0:ncn, :], in_=psb[6][0:ncn, 0:256], func=AF.Exp, scale=-1.0),
                     reads=[("ps", 6)], writes=["erev"])
                P.op(ACT, lambda e, ncn=ncn: e.activation(out=erev[0:ncn, :], in_=erev[0:ncn, :], func=AF.Ln, bias=1.0),
                     reads=["erev"], writes=["erev"])
                P.op(DVE, lambda e, ncn=ncn, ci=ci: e.tensor_scalar(out=g_tm[0:ncn, ci, :], in0=erev[0:ncn, :],
                                                                    scalar1=-1.0 / 16.0, scalar2=None, op0=ALU.mult),
                     reads=["erev"], writes=[("g_tm", ci)])
            def gen_chunks():
                ci = 0
                for (kind, seq, chunks) in gla_segs:
                    if kind == "s":
                        r0 = (seq * H_GLA + hg) * DK
                        P.op(SP, lambda e, r0=r0: e.dma_start(out=S[:, hg, :, :],
                                                              in_=st.ap()[r0:r0 + DK, :].rearrange("(k p) v -> p k v", p=128)),
                             writes=[("S", hg)], dma="S%d" % hg)
                        P.op(ACT, lambda e: e.activation(out=Sb[:, hg, :, :], in_=S[:, hg, :, :], func=AF.Copy),
                             reads=[("S", hg)], writes=[("Sb", hg)])
                    for (cc, ncn) in chunks:
                        gch = g_tm[0:ncn, ci, :]
                        for kc in range(2):
                            P.op(PE, lambda e, kc=kc, ncn=ncn, gch=gch: e.matmul(
                                psb[2][:, kc * 64:kc * 64 + ncn], lhsT=gch[:, kc * 128:(kc + 1) * 128], rhs=U32[0:ncn, 0:ncn],
                                start=True, stop=True), reads=[("g_tm", ci), "U32"], writes=[("ps", 2)])
                        P.op(PE, lambda e, ncn=ncn, gch=gch: e.matmul(psb[3][0:ncn, 0:256], lhsT=Ls32[0:ncn, 0:ncn], rhs=gch,
                                                                      start=True, stop=True),
                             reads=[("g_tm", ci), "Ls32"], writes=[("ps", 3)])
                        pc3 = psb[2][:, 0:128].rearrange("p (k t) -> p k t", k=2)[:, :, 0:ncn]
                        P.op(ACT, lambda e, ncn=ncn, pc3=pc3: e.activation(out=ecum[:, :, 0:ncn], in_=pc3, func=AF.Exp),
                             reads=[("ps", 2)], writes=["ecum"])
                        if not lite:
                            P.op(ACT, lambda e, ncn=ncn, pc3=pc3: e.activation(out=encum[:, :, 0:ncn], in_=pc3, func=AF.Exp, scale=-1.0),
                                 reads=[("ps", 2)], writes=["encum"])
                            P.op(DVE, lambda e, ncn=ncn, cc=cc: e.tensor_tensor(out=qd[:, :, 0:ncn], in0=qT[:, :, cc:cc + ncn],
                                                                                in1=ecum[:, :, 0:ncn], op=ALU.mult),
                                 reads=["qT", "ecum"], writes=["qd"])
                            P.op(DVE, lambda e, ncn=ncn, cc=cc: e.tensor_tensor(out=kd[:, :, 0:ncn], in0=kTg[:, :, cc:cc + ncn],
                                                                                in1=encum[:, :, 0:ncn], op=ALU.mult),
                                 reads=["kTg", "encum"], writes=["kd"])
                        P.op(ACT, lambda e, ncn=ncn: e.activation(out=erev[0:ncn, :], in_=psb[3][0:ncn, 0:256], func=AF.Exp),
                             reads=[("ps", 3)], writes=["erev"])
                        P.op(DVE, lambda e, ncn=ncn, ci=ci: e.tensor_tensor(out=kd2[0:ncn, :], in0=k_tm[0:ncn, ci, :],
                                                                            in1=erev[0:ncn, :], op=ALU.mult),
                             reads=[("k_tm", ci), "erev"], writes=["kd2"])
                        yield
                        for kc in range(0 if lite else 2):
                            P.op(PE, lambda e, kc=kc, ncn=ncn: e.matmul(psb[3][0:ncn, 256:256 + ncn], lhsT=kd[:, kc, 0:ncn],
                                                                        rhs=qd[:, kc, 0:ncn], start=(kc == 0), stop=(kc == 1)),
                                 reads=["kd", "qd"], writes=[("ps", 3)])
                        if not lite:
                            P.op(DVE, lambda e, ncn=ncn: e.tensor_tensor(out=attT[0:ncn, 0:ncn], in0=psb[3][0:ncn, 256:256 + ncn],
                                                                         in1=U32[0:ncn, 0:ncn], op=ALU.mult),
                                 reads=[("ps", 3), "U32"], writes=["attT"])
                        for dc in range(0 if lite else 4):
                            P.op(PE, lambda e, dc=dc, ncn=ncn, ci=ci: e.matmul(
                                psb[2][:, 128 + dc * 64:128 + dc * 64 + ncn], lhsT=v_tm[0:ncn, ci, dc * 128:(dc + 1) * 128],
                                rhs=attT[0:ncn, 0:ncn], start=True, stop=False),
                                reads=[("v_tm", ci), "attT"], writes=[("ps", 2)])
                            for kc in range(2):
                                P.op(PE, lambda e, dc=dc, kc=kc, ncn=ncn: e.matmul(
                                    psb[2][:, 128 + dc * 64:128 + dc * 64 + ncn], lhsT=Sb[:, hg, kc, dc * 128:(dc + 1) * 128],
                                    rhs=qd[:, kc, 0:ncn], start=False, stop=(kc == 1)),
                                    reads=[("Sb", hg), "qd"], writes=[("ps", 2)])
                        po3 = psb[2][:, 128:384].rearrange("p (d t) -> p d t", d=4)[:, :, 0:ncn]
                        if not lite:
                            P.op(ACT, lambda e, po3=po3, cc=cc, ncn=ncn: e.activation(out=A32[:, :, cc:cc + ncn], in_=po3, func=AF.Copy),
                                 reads=[("ps", 2)], writes=["A32"])
                        for kc in range(2):
                            P.op(PE, lambda e, kc=kc, ncn=ncn, ci=ci: e.matmul(
                                psb[3][:, 0:512] if False else psb[7][:, 0:512], lhsT=kd2[0:ncn, kc * 128:(kc + 1) * 128],
                                rhs=v_tm[0:ncn, ci, :], start=True, stop=True),
                                reads=["kd2", ("v_tm", ci)], writes=[("ps", 7)])
                            P.op(DVE, lambda e, kc=kc, ncn=ncn: e.scalar_tensor_tensor(
                                out=S[:, hg, kc, :], in0=S[:, hg, kc, :], scalar=ecum[:, kc, ncn - 1:ncn], in1=psb[7][:, 0:512],
                                op0=ALU.mult, op1=ALU.add), reads=[("S", hg), "ecum", ("ps", 7)], writes=[("S", hg)])
                        if (not lite) or ci == len(all_chunks) - 1:
                            P.op(ACT, lambda e: e.activation(out=Sb[:, hg, :, :], in_=S[:, hg, :, :], func=AF.Copy),
                                 reads=[("S", hg)], writes=[("Sb", hg)])
                        ci += 1
                        yield
                    if kind == "s":
                        r0 = (seq * H_GLA + hg) * DK
                        P.op(SP, lambda e, r0=r0: e.dma_start(out=o_gs.ap()[r0:r0 + DK, :].rearrange("(k p) v -> p k v", p=128),
                                                              in_=S[:, hg, :, :]), reads=[("S", hg)], dma="S%d" % hg)
                    elif kind == "p_last":
                        r0 = hg * DK
                        P.op(SP, lambda e, r0=r0: e.dma_start(out=o_gp.ap()[r0:r0 + DK, :].rearrange("(k p) v -> p k v", p=128),
                                                              in_=S[:, hg, :, :]), reads=[("S", hg)], dma="S%d" % hg)
            h0 = 4 * hg

            def gen_sbproj():
                wbuf, wkey = load_w(w_in_v, OFF_SQ + 512 * hg, 512)
                for h4 in range(4):
                    proj_fm(wbuf, wkey, h4 * 128, 128, N, lambda pt, pkey, h4=h4: P.op(
                        DVE, lambda e: e.scalar_tensor_tensor(out=QT4[:, h4, 0:N], in0=pt[:, 0:N], scalar=128 ** -0.5,
                                                              in1=rstd_bc[:, 0:N], op0=ALU.mult, op1=ALU.mult),
                        reads=[pkey, "rstd_bc"], writes=["QT4"]))
                    yield
                wbuf, wkey = load_w(w_in_v, OFF_SK + 512 * hg, 512)
                for h4 in range(4):
                    proj_fm(wbuf, wkey, h4 * 128, 128, N, lambda pt, pkey, h4=h4: P.op(
                        DVE, lambda e: e.tensor_tensor(out=KT4[:, h4, 0:N], in0=pt[:, 0:N], in1=rstd_bc[:, 0:N], op=ALU.mult),
                        reads=[pkey, "rstd_bc"], writes=["KT4"]))
                    yield
                kv_dst = []
                for (kind, seq, c0, NQ, pos0) in sb_segs:
                    scol = pos0 if kind != "s" else TKP + seq * TS
                    kv_dst.append((c0, NQ, scol))
                    for h4 in range(4):
                        h = h0 + h4
                        P.op(SP, lambda e, h=h, h4=h4, c0=c0, NQ=NQ, scol=scol: e.dma_start(
                            out=kts.ap()[h * 128:(h + 1) * 128, scol:scol + NQ], in_=KT4[:, h4, c0:c0 + NQ]),
                            reads=["KT4"], writes=[("kts", h)], dma="kts")
                sidx = [0]

                def tm_out(pt, pkey, ti, n, spec, is_v):
                    dst, r0, p0, cntr = spec
                    sgi = sidx[0] % 2
                    sidx[0] += 1
                    sg = stage[sgi]
                    P.op(ACT, lambda e: e.activation(out=sg[0:n, :], in_=pt[0:n, 0:512], func=AF.Copy,
                                                     scale=rstd_tm[0:n, ti:ti + 1]),
                         reads=[pkey, ("rstd", ti)], writes=[("stage", sgi)])
                    if dst is not None:
                        P.op(SP, lambda e: e.dma_start(out=dst.ap()[r0:r0 + cntr, 512 * hg:512 * hg + 512],
                                                       in_=sg[p0:p0 + cntr, :]),
                             reads=[("stage", sgi)], dma="stage%d" % sgi)
                    if is_v:
                        P.op(DVE, lambda e: e.tensor_copy(out=Vt[0:n, ti, :], in_=sg[0:n, :]),
                             reads=[("stage", sgi)], writes=[("Vt", ti)])

                for ti, (src, n, col0) in enumerate(tiles):
                    spec = out_rows[ti][1]
                    proj_tm(wbuf, wkey, 0, 512, col0, n, lambda pt, pkey, ti=ti, n=n, spec=spec: tm_out(
                        pt, pkey, ti, n, spec, False))
                    yield
                wbuf, wkey = load_w(w_in_v, OFF_SV + 512 * hg, 512)
                for ti, (src, n, col0) in enumerate(tiles):
                    spec = out_rows[ti][2]
                    proj_tm(wbuf, wkey, 0, 512, col0, n, lambda pt, pkey, ti=ti, n=n, spec=spec: tm_out(
                        pt, pkey, ti, n, spec, True))
                    yield
                    srow = out_rows[ti][3]
                    for h4 in range(4):
                        h = h0 + h4
                        P.op(SP, lambda e, h=h, h4=h4, ti=ti, n=n, srow=srow: e.dma_start(
                            out=vts.ap()[h * TKS + srow:h * TKS + srow + n, :], in_=Vt[0:n, ti, h4 * 128:(h4 + 1) * 128]),
                            reads=[("Vt", ti)], writes=[("vts", h)], dma="vts")
                wbuf, wkey = load_w(w_in_v, OFF_GB + 512 * hg, 512)
                for h4 in range(4):
                    def evb(pt, pkey, h4=h4):
                        P.op(DVE, lambda e: e.tensor_tensor(out=tmpB[:, 0:N], in0=pt[:, 0:N], in1=rstd_bc[:, 0:N], op=ALU.mult),
                             reads=[pkey, "rstd_bc"], writes=["tmpB"])
                        P.op(ACT, lambda e: e.activation(out=gbS[:, h4, 0:N], in_=tmpB[:, 0:N], func=AF.Sigmoid),
                             reads=["tmpB"], writes=["gbS"])
                    proj_fm(wbuf, wkey, h4 * 128, 128, N, evb)
                    yield

            _gens = [gen_chunks(), do_quad_sb_lite(hg) if lite else gen_sbproj()]
            while _gens:
                for _g in list(_gens):
                    try:
                        next(_g)
                    except StopIteration:
                        _gens.remove(_g)
            if lite:
                return
            P.op(ACT, lambda e: e.activation(out=sq16[:, :, 0:N], in_=A32[:, :, 0:N], func=AF.Square),
                 reads=["A32"], writes=["sq16"])
            for dc in range(4):
                P.op(PE, lambda e, dc=dc: e.matmul(psb[6][:, 0:N], lhsT=ones16[:, :], rhs=sq16[:, dc, 0:N],
                                                   start=(dc == 0), stop=(dc == 3)),
                     reads=["sq16", "ones16"], writes=[("ps", 6)])
            P.op(DVE, lambda e: e.tensor_scalar(out=tmpB[:, 0:N], in0=psb[6][:, 0:N], scalar1=1.0 / DV, scalar2=EPS,
                                                op0=ALU.mult, op1=ALU.add), reads=[("ps", 6)], writes=["tmpB"])
            P.op(ACT, lambda e: e.activation(out=tmpB[:, 0:N], in_=tmpB[:, 0:N], func=AF.Sqrt),
                 reads=["tmpB"], writes=["tmpB"])
            P.op(DVE, lambda e: e.reciprocal(out=tmpB[:, 0:N], in_=tmpB[:, 0:N]), reads=["tmpB"], writes=["tmpB"])
            for dc in range(4):
                gi_ = 48 + hg * 4 + dc
                P.op(DVE, lambda e, dc=dc, gi_=gi_: e.scalar_tensor_tensor(
                    out=A32[:, dc, 0:N], in0=A32[:, dc, 0:N], scalar=gT[:, gi_:gi_ + 1], in1=tmpB[:, 0:N],
                    op0=ALU.mult, op1=ALU.mult), reads=["A32", "gT", "tmpB"], writes=["A32"])
            P.op(DVE, lambda e: e.tensor_tensor(out=A32[:, :, 0:N], in0=A32[:, :, 0:N], in1=gaS[:, :, 0:N], op=ALU.mult),
                 reads=["A32", "gaS"], writes=["A32"])

            P.barrier(skip_dma=WBG4)

            def sb_job(sl, h4, seg):
                (kind, seq, c0, NQ, pos0) = seg
                h = h0 + h4
                B_ = SBB[sl]
                kt_, vt_, w16_, carry_, kc_, tB_ = (B_["kt"], B_["vt"], B_["w16"], B_["carry"], B_["kc"], B_["tB"])
                e32s = (B_["e32"], B_["e32b"])
                bz, bz2, bk, bo = (2, 3, 4, 5) if sl == 0 else (6, 7, 0, 1)
                K = lambda nm: (nm, sl)
                if kind == "s":
                    for half in range(NPT // 4 if NPT >= 4 else 1):
                        nt = min(4, NPT)
                        rbase = seq * PAST + half * 512
                        P.op(SP, lambda e, rbase=rbase, nt=nt: e.dma_start(
                            out=kc_[:, 0:nt, :],
                            in_=ck.ap()[rbase:rbase + nt * 128, h * 128:(h + 1) * 128].rearrange("(j p) d -> p j d", p=128)),
                            writes=[K("kc")], dma="kc%d" % sl)
                        for j in range(nt):
                            P.op(PE, lambda e, j=j: e.transpose(out=psb[bk][:, j * 128:(j + 1) * 128], in_=kc_[:, j, :],
                                                                identity=ident[:, :]),
                                 reads=[K("kc"), "ident"], writes=[("ps", bk)])
                        P.op(ACT, lambda e, half=half, nt=nt: e.activation(
                            out=kt_[:, half * 512:half * 512 + nt * 128], in_=psb[bk][:, 0:nt * 128], func=AF.Copy),
                            reads=[("ps", bk)], writes=[K("kt")])
                    rb = seq * PAST
                    P.op(POOL, lambda e, rb=rb: e.dma_start(
                        out=vt_[:, 0:NPT, :],
                        in_=cv.ap()[rb:rb + PAST, h * 128:(h + 1) * 128].rearrange("(j p) d -> p j d", p=128)),
                        writes=[K("vt")], dma="vt%d" % sl)
                    scol = TKP + seq * TS
                    P.op(SP, lambda e, scol=scol: e.dma_start(out=kt_[:, PAST:PAST + TS],
                                                              in_=kts.ap()[h * 128:(h + 1) * 128, scol:scol + TS]),
                         reads=[("kts", h)], writes=[K("kt")], dma="kt%d" % sl)
                    P.op(SP, lambda e, scol=scol: e.dma_start(out=vt_[0:TS, NPT, :],
                                                              in_=vts.ap()[h * TKS + scol:h * TKS + scol + TS, :]),
                         reads=[("vts", h)], writes=[K("vt")], dma="vt%d" % sl)
                    ktiles = [(128, None)] * NPT + [(TS, 0)]
                else:
                    kend = pos0 + NQ
                    nfull = kend // 128
                    rem = kend - nfull * 128
                    P.op(SP, lambda e, kend=kend: e.dma_start(out=kt_[:, 0:kend], in_=kts.ap()[h * 128:(h + 1) * 128, 0:kend]),
                         reads=[("kts", h)], writes=[K("kt")], dma="kt%d" % sl)
                    if nfull > 0:
                        P.op(SP, lambda e, nfull=nfull: e.dma_start(
                            out=vt_[:, 0:nfull, :],
                            in_=vts.ap()[h * TKS:h * TKS + nfull * 128, :].rearrange("(j p) d -> p j d", p=128)),
                            reads=[("vts", h)], writes=[K("vt")], dma="vt%d" % sl)
                    if rem > 0:
                        P.op(SP, lambda e, nfull=nfull, rem=rem: e.dma_start(
                            out=vt_[0:rem, nfull, :], in_=vts.ap()[h * TKS + nfull * 128:h * TKS + nfull * 128 + rem, :]),
                            reads=[("vts", h)], writes=[K("vt")], dma="vt%d" % sl)
                    ktiles = []
                    for j in range(nfull + (1 if rem else 0)):
                        kn = 128 if j < nfull else rem
                        d = j * 128 - pos0
                        if d + 127 < 0:
                            ktiles.append((kn, None))
                        else:
                            assert d % 128 == 0 and 0 <= d // 128 < 4
                            ktiles.append((kn, d // 128))
                QTh = QT4[:, h4, c0:c0 + NQ]
                P.op(DVE, lambda e: e.tensor_scalar(out=carry_[:, 0:NQ], in0=rstd_bc[:, 0:NQ], scalar1=0.0, scalar2=None,
                                                    op0=ALU.mult), reads=["rstd_bc"], writes=[K("carry")])
                yield
                nkt = len(ktiles)
                order = list(reversed(range(nkt)))
                bzs = (bz, bz2)
                arg_ = B_["arg"]

                def S1(t):
                    j = order[t]
                    kn, dg = ktiles[j]
                    p = t % 2
                    zb, eb = bzs[p], e32s[p]
                    ktj = kt_[:, j * 128:j * 128 + kn]
                    P.op(PE, lambda e: e.matmul(psb[zb][0:kn, 0:NQ], lhsT=ktj, rhs=QTh, start=True, stop=False,
                                                skip_group_check=True),
                         reads=[K("kt"), "QT4"], writes=[("ps", zb)])
                    if dg is not None:
                        P.op(PE, lambda e: e.matmul(psb[zb][0:kn, 0:NQ], lhsT=ident16[0:kn, 0:kn], rhs=masks[0:kn, dg, 0:NQ],
                                                    start=False, stop=False, skip_group_check=True),
                             reads=["ident16", "masks"], writes=[("ps", zb)])
                    P.op(ACT, lambda e: e.activation(out=tB_[0:kn, 0:NQ], in_=psb[zb][0:kn, 0:NQ], func=AF.Exp),
                         reads=[("ps", zb)], writes=[K("tB")])
                    P.op(ACT, lambda e: e.activation(out=eb[0:kn, 0:NQ], in_=tB_[0:kn, 0:NQ], func=AF.Ln, bias=1.0),
                         reads=[K("tB")], writes=[K(("e32", p))])

                def S2(t):
                    j = order[t]
                    kn, dg = ktiles[j]
                    p = t % 2
                    zb, eb = bzs[p], e32s[p]
                    P.op(PE, lambda e: e.matmul(psb[zb][:, 0:NQ], lhsT=negTr[0:kn, :], rhs=eb[0:kn, 0:NQ],
                                                start=False, stop=(t == 0), skip_group_check=True),
                         reads=["negTr", K(("e32", p))], writes=[("ps", zb)])
                    if t > 0:
                        P.op(PE, lambda e: e.matmul(psb[zb][:, 0:NQ], lhsT=nonesr[:, :], rhs=carry_[:, 0:NQ],
                                                    start=False, stop=True, skip_group_check=True),
                             reads=["nonesr", K("carry")], writes=[("ps", zb)])
                    P.op(ACT, lambda e: e.activation(out=w16_[0:kn, 0:NQ], in_=psb[zb][0:kn, 0:NQ], func=AF.Exp),
                         reads=[("ps", zb)], writes=[K("w16")])
                    if j > 0:
                        cf, ef = carry_.bitcast(F32), eb.bitcast(F32)
                        P.op(DVE, lambda e: e.tensor_tensor(out=carry_[0:kn, 0:NQ], in0=cf[0:kn, 0:NQ],
                                                            in1=ef[0:kn, 0:NQ], op=ALU.add),
                             reads=[K("carry"), K(("e32", p))], writes=[K("carry")])

                def S3(t):
                    j = order[t]
                    kn, dg = ktiles[j]
                    P.op(PE, lambda e: e.matmul(psb[bo][:, 0:NQ], lhsT=vt_[0:kn, j, :], rhs=w16_[0:kn, 0:NQ],
                                                start=(t == 0), stop=(t == nkt - 1)),
                         reads=[K("vt"), K("w16")], writes=[("ps", bo)])

                for k in range(nkt + 2):
                    if 0 <= k - 2 < nkt:
                        S3(k - 2)
                        yield
                    if k < nkt:
                        S1(k)
                        yield
                    if 0 <= k - 1 < nkt:
                        S2(k - 1)
                        yield

                P.op(DVE, lambda e: e.tensor_tensor(out=tB_[:, 0:NQ], in0=psb[bo][:, 0:NQ],
                                                    in1=gbS[:, h4, c0:c0 + NQ], op=ALU.mult),
                     reads=[("ps", bo), "gbS"], writes=[K("tB")])
                P.op(DVE, lambda e: e.tensor_tensor(
                    out=mixT[:, h, c0:c0 + NQ], in0=tB_[:, 0:NQ], in1=A32[:, h4, c0:c0 + NQ], op=ALU.add),
                    reads=[K("tB"), "A32"], writes=["mixT"])

            jobs = [(h4, seg) for h4 in range(4) for seg in sb_segs]
            active = [None, None]
            while jobs or any(a is not None for a in active):
                for sl in range(2):
                    if active[sl] is None and jobs:
                        h4_, seg_ = jobs.pop(0)
                        active[sl] = sb_job(sl, h4_, seg_)
                    if active[sl] is not None:
                        try:
                            next(active[sl])
                        except StopIteration:
                            active[sl] = None
            P.barrier(skip_dma=WBG4)


        for hg_ in range(H_GLA):
            if lite:
                do_quad_lite(hg_)
            else:
                do_quad(hg_)
        if lite:
            return

        A.reset(base_mark)
        xacc = A([128, 4, D], F32)
        hT = A([128, 32, 512], BF16)
        gf_bc = A([128, D], F32)
        yo = A([128, D], F32)
        del wbs[2:]
        while A.top - A.mark() >= 16384 + 64 and len(wbs) < 4:
            wbs.append(A([128, NCH, 512], BF16))
        P.barrier(skip_dma=WBG2)
        P.op(SP, lambda e: e.dma_start(out=yo[0:1, :], in_=gf_row.ap()), writes=["yo"], dma="yo")
        for q in range(4):
            P.op(PE, lambda e, q=q: e.matmul(psb[6][:, 0:512], lhsT=ones32[0:1, :], rhs=yo[0:1, q * 512:(q + 1) * 512],
                                             start=True, stop=True), reads=["yo", "ones32"], writes=[("ps", 6)])
            P.op(DVE, lambda e, q=q: e.tensor_copy(out=gf_bc[:, q * 512:(q + 1) * 512], in_=psb[6][:, 0:512]),
                 reads=[("ps", 6)], writes=["gf_bc"])
        for cb in range(4):
            wbuf, wkey = load_w(w_out_v, cb * 512, 512)
            for ti, (src, n, col0) in enumerate(tiles):
                if cb == 0:
                    pass
                P.op(SP, lambda e, src=src, n=n, cb=cb: e.dma_start(out=xin[0:n, 0:512], in_=src[:, cb * 512:(cb + 1) * 512]),
                     writes=["xin"], dma="xin")
                pt, pkey = ps_gen()
                for c in range(NCH):
                    P.op(PE, lambda e, c=c, pt=pt, n=n, col0=col0, wbuf=wbuf: e.matmul(
                        pt[0:n, 0:512], lhsT=mixT[:, c, col0:col0 + n], rhs=wbuf[:, c, 0:512],
                        start=(c == 0), stop=(c == NCH - 1)), reads=[wkey, "mixT"], writes=[pkey])
                P.op(DVE, lambda e, pt=pt, n=n, ti=ti, cb=cb: e.tensor_tensor(
                    out=xacc[0:n, ti, cb * 512:(cb + 1) * 512], in0=pt[0:n, 0:512], in1=xin[0:n, 0:512], op=ALU.add),
                    reads=[pkey, "xin"], writes=[("xacc", ti)])
        norm_T(tiles, N, 16, src_is_acc=xacc)
        for hh in range(2):
            for hb in range(8):
                wbuf, wkey = load_w(w_up_v, hh * 4096 + hb * 512, 512)
                for hc in range(4):
                    def evm(pt, pkey, idx=hb * 4 + hc):
                        P.op(DVE, lambda e: e.tensor_tensor(out=yo[:, 0:N], in0=pt[:, 0:N], in1=rstd_bc[:, 0:N], op=ALU.mult),
                             reads=[pkey, "rstd_bc"], writes=["yo"])
                        P.op(DVE, lambda e: e.scalar_tensor_tensor(out=hT[:, idx, 0:N], in0=yo[:, 0:N], scalar=0.0,
                                                                   in1=yo[:, 0:N], op0=ALU.max, op1=ALU.mult),
                             reads=["yo"], writes=["hT"])
                    proj_fm(wbuf, wkey, hc * 128, 128, N, evm)
            for cb in range(8):
                i = cnt["wb"] % len(wbs)
                cnt["wb"] += 1
                buf = wbs[i]
                wkey = ("wb", i)
                wdv = w_down.ap()[hh * 4096:(hh + 1) * 4096, cb * 256:(cb + 1) * 256].rearrange("(c p) m -> p c m", p=128)
                bview = buf[:].rearrange("p c m -> p (c m)")[:, 0:32 * 256].rearrange("p (c m) -> p c m", m=256)
                P.op(POOL, lambda e, bview=bview, wdv=wdv: e.dma_start(out=bview, in_=wdv), writes=[wkey], dma="wb%d" % i,
                     nobar=(i < 2))
                for ti, (src, n, col0) in enumerate(tiles):
                    pt, pkey = ps_gen()
                    for c in range(32):
                        P.op(PE, lambda e, c=c, pt=pt, n=n, col0=col0, bview=bview: e.matmul(
                            pt[0:n, 0:256], lhsT=hT[:, c, col0:col0 + n], rhs=bview[:, c, :],
                            start=(c == 0), stop=(c == 31)), reads=[wkey, "hT"], writes=[pkey])
                    P.op(DVE, lambda e, pt=pt, n=n, ti=ti, cb=cb: e.tensor_tensor(
                        out=xacc[0:n, ti, cb * 256:(cb + 1) * 256], in0=pt[0:n, 0:256],
                        in1=xacc[0:n, ti, cb * 256:(cb + 1) * 256], op=ALU.add),
                        reads=[pkey, ("xacc", ti)], writes=[("xacc", ti)])
        for ti, (src, n, col0) in enumerate(tiles):
            P.op(ACT, lambda e, n=n, ti=ti: e.activation(out=junk[0:n, 0:D], in_=xacc[0:n, ti, :], func=AF.Square,
                                                         accum_out=ss[0:n, ti:ti + 1]),
                 reads=[("xacc", ti)], writes=["mixT", ("ss", ti)])
            P.op(DVE, lambda e, n=n, ti=ti: e.tensor_scalar(out=rstd_tm[0:n, ti:ti + 1], in0=ss[0:n, ti:ti + 1],
                                                            scalar1=1.0 / D, scalar2=EPS, op0=ALU.mult, op1=ALU.add),
                 reads=[("ss", ti)], writes=[("rstd", ti)])
            P.op(ACT, lambda e, n=n, ti=ti: e.activation(out=rstd_tm[0:n, ti:ti + 1], in_=rstd_tm[0:n, ti:ti + 1],
                                                         func=AF.Sqrt), reads=[("rstd", ti)], writes=[("rstd", ti)])
            P.op(DVE, lambda e, n=n, ti=ti: e.reciprocal(out=rstd_tm[0:n, ti:ti + 1], in_=rstd_tm[0:n, ti:ti + 1]),
                 reads=[("rstd", ti)], writes=[("rstd", ti)])
            P.op(DVE, lambda e, n=n, ti=ti: e.scalar_tensor_tensor(out=yo[0:n, :], in0=xacc[0:n, ti, :],
                                                                   scalar=rstd_tm[0:n, ti:ti + 1], in1=gf_bc[0:n, :],
                                                                   op0=ALU.mult, op1=ALU.mult),
                 reads=[("xacc", ti), ("rstd", ti), "gf_bc"], writes=["yo"])
            dst, r0, p0, cnt_rows = out_rows[ti][0]
            if cnt_rows > 0:
                P.op(SP, lambda e, dst=dst, r0=r0, p0=p0, cnt_rows=cnt_rows: e.dma_start(
                    out=dst.ap()[r0:r0 + cnt_rows, :], in_=yo[p0:p0 + cnt_rows, :]), reads=["yo"], dma="yo")

    A.reset(base_mark)
    rs_tok = A([64, 16], F32)
    base_mark = A.mark()

    NONE4 = (None, 0, 0, 0)
    meta_spec = lambda dst: (dst, 0, 128 - N_META, N_META)
    process_group(-1, [(xp.ap()[0:128, :], 128, 0)], 128, [("p", 0, [(0, 64), (64, 64)])], [("p", 0, 0, 128, 0)],
                  [(NONE4, meta_spec(o_kp) if NGP == 0 else NONE4, meta_spec(o_vp) if NGP == 0 else NONE4, 0)], lite=True)
    for g in range(NG_ALL):
        tiles, out_rows = [], []
        lite = g < NGP
        base = 128 + g * 512
        for t in range(4):
            tok0 = base + t * 128
            tiles.append((xp.ap()[tok0:tok0 + 128, :], 128, t * 128))
            if lite:
                last16 = (g == NGP - 1 and t == 3)
                out_rows.append((NONE4, meta_spec(o_kp) if last16 else NONE4, meta_spec(o_vp) if last16 else NONE4, tok0))
            else:
                loc = (g - NGP) * 512 + t * 128
                out_rows.append(((o_yp, loc, 0, 128), (o_kp, N_META + loc, 0, 128), (o_vp, N_META + loc, 0, 128), tok0))
        chunks = [(c * 64, 64) for c in range(8)]
        kind = "p_last" if g == NG_ALL - 1 else "p"
        process_group(g, tiles, 512, [(kind, 0, chunks)], [("p", 0, 0, 512, base)], out_rows, lite=lite)
    tiles, out_rows, gla_segs, sb_segs = [], [], [], []
    col = 0
    for sq_ in range(NS):
        tiles.append((xsm.ap()[sq_ * TS:(sq_ + 1) * TS, :], TS, col))
        out_rows.append(((o_ys, sq_ * TS, 0, TS), (o_ks, sq_ * TS, 0, TS), (o_vs, sq_ * TS, 0, TS), TKP + sq_ * TS))
        gla_segs.append(("s", sq_, [(col, TS)]))
        sb_segs.append(("s", sq_, col, TS, PAST))
        col += TS
    process_group(NG_ALL, tiles, col, gla_segs, sb_segs, out_rows)

    P.emit()
    es.close()
    return nc


_NC_CACHE = {}


def _run(inputs, SEQ, PAST, n_cores=8):
    key = (SEQ, PAST)
    if key not in _NC_CACHE:
        _NC_CACHE[key] = build_nc(SEQ, PAST)
    nc = _NC_CACHE[key]
    f = lambda a: np.ascontiguousarray(np.asarray(a, dtype=np.float32))
    x_prompt, x_sample = f(inputs["x_prompt"]), f(inputs["x_sample"])
    ckk, cvv, stt = f(inputs["cache_sb_k"])[0], f(inputs["cache_sb_v"])[0], f(inputs["state_gla"])[0]
    meta = f(inputs["meta_tokens"])
    gvec = np.concatenate([f(inputs["norm1_g"]).reshape(16, 128), f(inputs["norm2_g"]).reshape(16, 128),
                           f(inputs["norm_f_g"]).reshape(16, 128), f(inputs["gla_norm_g"]).reshape(16, 128)], 0)
    shared = {
        "w_in": f(inputs["w_in"])[0], "w_out": f(inputs["w_out"])[0], "w_up": f(inputs["w_up"])[0],
        "w_down": f(inputs["w_down"])[0], "w_al": f(inputs["w_alpha_up"])[0], "b_al": f(inputs["b_alpha"]).reshape(1, 1024),
        "gvec": np.ascontiguousarray(gvec), "gf_row": f(inputs["norm_f_g"]).reshape(1, D),
    }
    B = x_prompt.shape[0]
    NG_ALL = SEQ // 512
    NSP = 4 if NG_ALL % 4 == 0 else 1
    OWN = (NG_ALL // NSP) * 512
    E = OWN * (NSP - 1)
    T_P = N_META + SEQ
    core_bj = []
    in_maps = []
    for c in range(n_cores):
        b, j = (c // NSP, c % NSP) if NSP * B == n_cores else (c % B, 0)
        core_bj.append((b, j))
        m = dict(shared)
        m["xp"] = np.ascontiguousarray(np.concatenate(
            [np.zeros((128 - N_META + E - OWN * j, D), np.float32), meta, x_prompt[b][0:OWN * (j + 1)]], 0))
        m["xs"] = np.ascontiguousarray(x_sample[NS * c:NS * c + NS].reshape(NS * TS, D))
        m["ck"] = np.ascontiguousarray(ckk[NS * c:NS * c + NS].reshape(NS * PAST, D))
        m["cv"] = np.ascontiguousarray(cvv[NS * c:NS * c + NS].reshape(NS * PAST, D))
        m["st"] = np.ascontiguousarray(stt[NS * c:NS * c + NS].reshape(NS * H_GLA * DK, DV))
        in_maps.append(m)
    res = run_bass_kernel_spmd(nc, in_maps, core_ids=list(range(n_cores)))
    R = res.results
    yf = np.zeros((B, SEQ, D), np.float32)
    kf = np.zeros((B, T_P, D), np.float32)
    vf = np.zeros((B, T_P, D), np.float32)
    gla_p = np.zeros((1, B, H_GLA, DK, DV), np.float32)
    for c, (b, j) in enumerate(core_bj):
        if NSP * B != n_cores and c >= B:
            continue
        yf[b, OWN * j:OWN * (j + 1)] = R[c]["o_yp"]
        for dst, nm in ((kf, "o_kp"), (vf, "o_vp")):
            dst[b, N_META + OWN * j:N_META + OWN * (j + 1)] = R[c][nm][N_META:]
            if j == 0:
                dst[b, 0:N_META] = R[c][nm][0:N_META]
        if j == NSP - 1:
            gla_p[0, b] = R[c]["o_gp"].reshape(H_GLA, DK, DV)
    y_prompt = yf
    k_p = kf.reshape(1, B, T_P, H_SB, 128)
    v_p = vf.reshape(1, B, T_P, H_SB, 128)
    y_sample = np.concatenate([R[c]["o_ys"].reshape(NS, TS, D) for c in range(n_cores)], 0)
    gla_s = np.concatenate([R[c]["o_gs"].reshape(NS, H_GLA, DK, DV) for c in range(n_cores)], 0)[None]
    k_s = np.concatenate([R[c]["o_ks"].reshape(NS, TS, H_SB, 128) for c in range(n_cores)], 0)[None]
    v_s = np.concatenate([R[c]["o_vs"].reshape(NS, TS, H_SB, 128) for c in range(n_cores)], 0)[None]
    return (y_prompt, y_sample, gla_p, k_p, v_p, gla_s, k_s, v_s)


def kernel(**inputs):
    SEQ = int(np.asarray(inputs["x_prompt"]).shape[1])
    PAST = int(np.asarray(inputs["cache_sb_k"]).shape[2])
    return _run(inputs, SEQ, PAST)
```

```python
import numpy as np
import concourse.bass as bass
import concourse.mybir as mybir
from concourse.bass_utils import run_bass_kernel_spmd
from contextlib import ExitStack

F32 = mybir.dt.float32
BF16 = mybir.dt.bfloat16
F32R = mybir.dt.float32r
ALU = mybir.AluOpType
AF = mybir.ActivationFunctionType

PE, ACT, DVE, POOL, SP = "tensor", "scalar", "vector", "gpsimd", "sync"
ENGS = (PE, ACT, DVE, POOL, SP)

D = 2048
NCH = 16
H_SB = 16
H_GLA = 4
DK = 256
DV = 512
DFF = 8192
N_META = 16
EPS = 1e-5
TS = 32
NS = 2
D_IN = 14352
OFF_GK, OFF_GV, OFF_ALOW, OFF_SQ, OFF_SK, OFF_SV, OFF_GA, OFF_GB = 1024, 2048, 4096, 4112, 6160, 8208, 10256, 12304
SBUF_BASE = 16512
SBUF_TOP = 229376


class Op:
    __slots__ = ("eng", "emit", "waits", "signal", "semval", "dma_grp", "dma_val")

    def __init__(self, eng, emit, dma_grp=None):
        self.eng = eng
        self.emit = emit
        self.waits = []
        self.signal = False
        self.semval = None
        self.dma_grp = dma_grp
        self.dma_val = None


class Prog:
    def __init__(self, nc):
        self.nc = nc
        self.ops = {e: [] for e in ENGS}
        self.last_write = {}
        self.readers = {}
        self.dma_count = {}
        self.dma_cons = {}
        self.bar = None
        self.bar_pending = set()

    def barrier(self, skip_dma=()):
        b = []
        for e in ENGS:
            for o in reversed(self.ops[e]):
                if o.dma_grp is None:
                    o.signal = True
                    b.append(("eng", o))
                    break
        for g, c in self.dma_count.items():
            if g not in skip_dma:
                b.append(("dma", g, 16 * c))
        self.bar = b
        self.bar_pending = set(ENGS)

    def op(self, eng, emit, reads=(), writes=(), dma=None, nobar=False):
        o = Op(eng, emit, dma_grp=dma)
        deps = []
        for r in reads:
            lw = self.last_write.get(r)
            if lw is not None:
                deps.append((lw, "raw"))
        for w in writes:
            lw = self.last_write.get(w)
            if lw is not None:
                deps.append((lw, "waw"))
            for rd in self.readers.get(w, ()):
                deps.append((rd, "war"))
        seen = set()
        for d, kind in deps:
            if d is o:
                continue
            if d.dma_grp is not None:
                key = ("dma", d.dma_grp)
                if key in seen:
                    continue
                seen.add(key)
                o.waits.append(("dma", d.dma_grp, 16 * self.dma_count[d.dma_grp]))
                self.dma_cons.setdefault(d.dma_grp, []).append(o)
                continue
            if d.eng == eng and dma is None:
                if eng == PE:
                    continue
            if id(d) in seen:
                continue
            seen.add(id(d))
            d.signal = True
            o.waits.append(("eng", d))
        if eng in self.bar_pending and not nobar:
            self.bar_pending.discard(eng)
            for w in self.bar:
                if w[0] == "eng" and w[1].eng == eng and dma is None:
                    continue
                o.waits.append(w)
        for r in reads:
            self.readers.setdefault(r, []).append(o)
        for w in writes:
            self.last_write[w] = o
            self.readers[w] = []
        if dma is not None:
            for c in self.dma_cons.pop(dma, ()):
                if c is o or c.eng == eng:
                    continue
                if c.dma_grp is not None:
                    o.waits.append(("dma", c.dma_grp, c.dma_val))
                else:
                    c.signal = True
                    o.waits.append(("eng", c))
            self.dma_count[dma] = self.dma_count.get(dma, 0) + 1
            o.dma_val = 16 * self.dma_count[dma]
        self.ops[eng].append(o)
        return o

    def emit(self):
        nc = self.nc
        with ExitStack() as es:
            esem = {e: es.enter_context(nc.semaphore("sem_" + e)) for e in (PE, ACT, DVE, POOL)}
            dsem = {g: es.enter_context(nc.semaphore("dsem_" + str(g))) for g in self.dma_count}
            for e in ENGS:
                c = 0
                for o in self.ops[e]:
                    if o.dma_grp is None and o.signal:
                        c += 1
                        o.semval = c
            block = es.enter_context(nc.Block())

            def run(engname):
                def body(eng):
                    waited = {}
                    for o in self.ops[engname]:
                        for w in o.waits:
                            if w[0] == "dma":
                                sem, val, key = dsem[w[1]], w[2], ("d", w[1])
                            else:
                                d = w[1]
                                sem, val, key = esem[d.eng], d.semval, ("e", d.eng)
                            if waited.get(key, 0) >= val:
                                continue
                            waited[key] = val
                            eng.wait_ge(sem, val)
                        ins = o.emit(eng)
                        if o.dma_grp is not None:
                            ins.then_inc(dsem[o.dma_grp], 16)
                        elif o.signal:
                            ins.then_inc(esem[engname], 1)
                    if engname == SP:
                        for g, c in self.dma_count.items():
                            eng.wait_ge(dsem[g], 16 * c)
                return body

            block.tensor(run(PE))
            block.scalar(run(ACT))
            block.vector(run(DVE))
            block.gpsimd(run(POOL))
            block.sync(run(SP))


class Alloc:
    def __init__(self, nc, base, top):
        self.nc, self.off, self.top = nc, base, top
        self.n = 0

    def mark(self):
        return self.off

    def reset(self, off):
        self.off = off

    def __call__(self, shape, dt):
        per = 1
        for s in shape[1:]:
            per *= s
        nbytes = per * (4 if dt in (F32, F32R) else 2)
        nbytes = (nbytes + 63) // 64 * 64
        assert self.off + nbytes <= self.top, ("SBUF overflow", self.off, nbytes, self.top)
        self.n += 1
        t = self.nc.alloc_sbuf_tensor_at("t%d" % self.n, list(shape), dt, offset=self.off)
        self.off += nbytes
        return t


def build_nc(SEQ, PAST):
    assert SEQ % 512 == 0 and PAST % 128 == 0
    NG_ALL = SEQ // 512
    NSP = 4 if NG_ALL % 4 == 0 else 1
    OWN_G = NG_ALL // NSP
    NGP = NG_ALL - OWN_G
    OWN = OWN_G * 512
    IDX = 128 + SEQ
    T_P = IDX
    TKP = IDX
    NKT_P = TKP // 128
    NPT = PAST // 128
    NTOK_S = NS * TS

    nc = bass.Bass("TRN2", target_bir_lowering=False)
    dr = lambda name, shape, dt=F32, kind="ExternalInput": nc.dram_tensor(name, list(shape), dt, kind=kind)
    xp = dr("xp", [T_P, D])
    xsm = dr("xs", [NTOK_S, D])
    ck = dr("ck", [NS * PAST, D])
    cv = dr("cv", [NS * PAST, D])
    st = dr("st", [NS * H_GLA * DK, DV])
    w_in = dr("w_in", [D, D_IN])
    w_out = dr("w_out", [D, D])
    w_up = dr("w_up", [D, DFF])
    w_down = dr("w_down", [DFF, D])
    w_al = dr("w_al", [16, 1024])
    b_al = dr("b_al", [1, 1024])
    gvec = dr("gvec", [4 * 16, 128])
    gf_row = dr("gf_row", [1, D])
    o_yp = dr("o_yp", [OWN, D], kind="ExternalOutput")
    o_ys = dr("o_ys", [NTOK_S, D], kind="ExternalOutput")
    o_gp = dr("o_gp", [H_GLA * DK, DV], kind="ExternalOutput")
    o_kp = dr("o_kp", [N_META + OWN, D], kind="ExternalOutput")
    o_vp = dr("o_vp", [N_META + OWN, D], kind="ExternalOutput")
    o_gs = dr("o_gs", [NS * H_GLA * DK, DV], kind="ExternalOutput")
    o_ks = dr("o_ks", [NTOK_S, D], kind="ExternalOutput")
    o_vs = dr("o_vs", [NTOK_S, D], kind="ExternalOutput")
    TKS = TKP + 128
    kts = nc.dram_tensor("kts", [H_SB * 128, TKS], BF16)
    vts = nc.dram_tensor("vts", [H_SB * TKS, 128], BF16)

    P = Prog(nc)
    A = Alloc(nc, SBUF_BASE, SBUF_TOP)
    es = ExitStack()
    psb = [es.enter_context(nc.psum_tensor("ps%d" % i, [128, 512], F32)) for i in range(8)]

    ident = A([128, 128], F32)
    ones32 = A([128, 128], F32)
    ones16 = A([128, 128], BF16)
    negT = A([128, 128], BF16)
    negTr = A([128, 128], F32R)
    onesr = A([128, 128], F32R)
    nonesr = A([128, 128], F32R)
    U32 = A([128, 64], F32)
    LsHi = A([128, 128], F32)
    Ls32 = A([64, 64], F32)
    masks = A([128, 4, 512], BF16)
    gT = A([128, 64], F32)
    gtmp = A([64, 128], F32)
    walp = A([17, 1024], F32)
    S = A([128, H_GLA, 2, DV], F32)
    Sb = A([128, H_GLA, 2, DV], BF16)
    xin = A([128, D], F32)
    xT = A([128, NCH, 512], BF16)
    rstd_bc = A([128, 512], F32)
    rstd_tm = A([128, 8], F32)
    ss = A([128, 8], F32)
    rrow = A([1, 128], F32)
    wb = [A([128, NCH, 512], BF16) for _ in range(2)]
    mixT = A([128, NCH, 512], BF16)
    junk = mixT[:].rearrange("p c n -> p (c n)")
    base_mark = A.mark()

    cnt = {"wb": 0, "ps": 0}
    WBG2 = ("wb0", "wb1")
    WBG4 = ("wb0", "wb1", "wb2", "wb3")
    wbs = list(wb)

    def ps_gen():
        i = cnt["ps"] % 2
        cnt["ps"] += 1
        return psb[i], ("ps", i)

    P.op(POOL, lambda e: e.memset(ident[:], 0.0), writes=["ident"])
    P.op(POOL, lambda e: e.affine_select(out=ident[:], in_=ident[:], pattern=[[-1, 128]], base=0, channel_multiplier=1,
                                          compare_op=ALU.not_equal, fill=1.0), reads=["ident"], writes=["ident"])
    P.op(POOL, lambda e: e.memset(ones32[:], 1.0), writes=["ones32"])
    P.op(POOL, lambda e: e.memset(ones16[:], 1.0), writes=["ones16"])
    P.op(POOL, lambda e: e.memset(negT[:], -1.0), writes=["negT"])
    P.op(POOL, lambda e: e.affine_select(out=negT[:], in_=negT[:], pattern=[[-1, 128]], base=0, channel_multiplier=1,
                                          compare_op=ALU.is_ge, fill=0.0), reads=["negT"], writes=["negT"])
    P.op(DVE, lambda e: e.tensor_copy(out=negTr[:], in_=negT[:]), reads=["negT"], writes=["negTr"])
    P.op(DVE, lambda e: e.tensor_copy(out=onesr[:], in_=ones32[:]), reads=["ones32"], writes=["onesr"])
    P.op(DVE, lambda e: e.tensor_scalar(out=nonesr[:], in0=ones32[:], scalar1=-1.0, scalar2=None, op0=ALU.mult),
         reads=["ones32"], writes=["nonesr"])
    P.op(POOL, lambda e: e.memset(U32[0:64, :], 1.0), writes=["U32"])
    P.op(POOL, lambda e: e.affine_select(out=U32[0:64, :], in_=U32[0:64, :], pattern=[[1, 64]], base=0, channel_multiplier=-1,
                                          compare_op=ALU.is_ge, fill=0.0), reads=["U32"], writes=["U32"])
    P.op(SP, lambda e: e.dma_start(out=U32[64:128, :], in_=U32[0:64, :]), reads=["U32"], writes=["U32hi"], dma="c3")
    P.op(POOL, lambda e: e.memset(LsHi[:], 0.0), writes=["LsHi"])
    P.op(POOL, lambda e: e.memset(Ls32[:], 1.0), writes=["Ls32"])
    P.op(POOL, lambda e: e.affine_select(out=Ls32[:], in_=Ls32[:], pattern=[[-1, 64]], base=0, channel_multiplier=1,
                                          compare_op=ALU.is_gt, fill=0.0), reads=["Ls32"], writes=["Ls32"])
    P.op(SP, lambda e: e.dma_start(out=LsHi[64:128, 64:128], in_=Ls32[0:64, 0:64]), reads=["Ls32", "LsHi"], writes=["LsHi"], dma="c4")
    P.op(POOL, lambda e: e.memset(masks[:], 1.0), writes=["masks"])
    for i in range(4):
        P.op(POOL, lambda e, i=i: e.affine_select(out=masks[:, i, :], in_=masks[:, i, :], pattern=[[1, 512]],
                                                  base=-128 * i, channel_multiplier=-1, compare_op=ALU.is_gt, fill=0.0),
             reads=["masks"], writes=["masks"])
    P.op(SP, lambda e: e.dma_start(out=gtmp[:], in_=gvec.ap()), writes=["gtmp"], dma="c0")
    P.op(PE, lambda e: e.transpose(out=psb[7][:, 0:64], in_=gtmp[:], identity=ident[0:64, 0:64]),
         reads=["gtmp", "ident"], writes=[("ps", 7)])
    P.op(DVE, lambda e: e.tensor_copy(out=gT[:], in_=psb[7][:, 0:64]), reads=[("ps", 7)], writes=["gT"])
    P.op(SP, lambda e: e.dma_start(out=walp[0:16, :], in_=w_al.ap()), writes=["walpw"], dma="c1")
    P.op(SP, lambda e: e.dma_start(out=walp[16:17, :], in_=b_al.ap()), writes=["walpb"], dma="c2")
    P.op(DVE, lambda e: e.memset(S[:], 0.0), writes=[("S", h) for h in range(4)])
    P.op(DVE, lambda e: e.memset(Sb[:], 0.0), writes=[("Sb", h) for h in range(4)])

    w_in_v = w_in.ap().rearrange("(c p) m -> p c m", p=128)
    w_out_v = w_out.ap().rearrange("(c p) m -> p c m", p=128)
    w_up_v = w_up.ap().rearrange("(c p) m -> p c m", p=128)

    def load_w(view, c0, m, part=0):
        i = cnt["wb"] % len(wbs)
        cnt["wb"] += 1
        buf = wbs[i]
        P.op(POOL, lambda e: e.dma_start(out=buf[:, :, 0:m], in_=view[:, :, c0:c0 + m]),
             writes=[("wb", i)], dma="wb%d" % i, nobar=(i < 2))
        return buf, ("wb", i)

    def norm_T(tiles, N, gcol, src_is_acc=None):
        for ti, (src, n, col0) in enumerate(tiles):
            if src_is_acc is None:
                P.op(SP, lambda e, src=src, n=n: e.dma_start(out=xin[0:n, :], in_=src), writes=["xin"], dma="xin")
                xs_ap = xin[0:n, :]
                xkey = "xin"
            else:
                xs_ap = src_is_acc[0:n, ti, :]
                xkey = ("xacc", ti)
            P.op(ACT, lambda e, xs_ap=xs_ap, n=n, ti=ti: e.activation(out=junk[0:n, 0:D], in_=xs_ap, func=AF.Square,
                                                                      accum_out=ss[0:n, ti:ti + 1]),
                 reads=[xkey], writes=["mixT", ("ss", ti)])
            P.op(DVE, lambda e, n=n, ti=ti: e.tensor_scalar(out=rstd_tm[0:n, ti:ti + 1], in0=ss[0:n, ti:ti + 1],
                                                            scalar1=1.0 / D, scalar2=EPS, op0=ALU.mult, op1=ALU.add),
                 reads=[("ss", ti)], writes=[("rstd", ti)])
            P.op(ACT, lambda e, n=n, ti=ti: e.activation(out=rstd_tm[0:n, ti:ti + 1], in_=rstd_tm[0:n, ti:ti + 1],
                                                         func=AF.Sqrt), reads=[("rstd", ti)], writes=[("rstd", ti)])
            P.op(DVE, lambda e, n=n, ti=ti: e.reciprocal(out=rstd_tm[0:n, ti:ti + 1], in_=rstd_tm[0:n, ti:ti + 1]),
                 reads=[("rstd", ti)], writes=[("rstd", ti)])
            for q4 in range(4):
                bank = 4 + (q4 % 2)
                for cc in range(4):
                    c = q4 * 4 + cc
                    P.op(PE, lambda e, bank=bank, cc=cc, c=c, n=n, xs_ap=xs_ap: e.transpose(
                        out=psb[bank][:, cc * 128:cc * 128 + n], in_=xs_ap[:, c * 128:(c + 1) * 128],
                        identity=ident[0:n, 0:n]), reads=[xkey, "ident"], writes=[("ps", bank)])
                for cc in range(4):
                    c = q4 * 4 + cc
                    eng = ACT if cc % 2 == 0 else DVE
                    if eng == ACT:
                        P.op(ACT, lambda e, bank=bank, cc=cc, c=c, n=n, col0=col0: e.activation(
                            out=xT[:, c, col0:col0 + n], in_=psb[bank][:, cc * 128:cc * 128 + n], func=AF.Copy,
                            scale=gT[:, gcol + c:gcol + c + 1]), reads=[("ps", bank), "gT"], writes=["xT"])
                    else:
                        P.op(DVE, lambda e, bank=bank, cc=cc, c=c, n=n, col0=col0: e.tensor_scalar(
                            out=xT[:, c, col0:col0 + n], in0=psb[bank][:, cc * 128:cc * 128 + n],
                            scalar1=gT[:, gcol + c:gcol + c + 1], scalar2=None, op0=ALU.mult),
                            reads=[("ps", bank), "gT"], writes=["xT"])
            P.op(PE, lambda e, n=n, ti=ti: e.transpose(out=psb[6][0:1, 0:n], in_=rstd_tm[0:n, ti:ti + 1],
                                                       identity=ident[0:n, 0:n]),
                 reads=[("rstd", ti), "ident"], writes=[("ps", 6)])
            P.op(DVE, lambda e, n=n: e.tensor_copy(out=rrow[0:1, 0:n], in_=psb[6][0:1, 0:n]),
                 reads=[("ps", 6)], writes=["rrow"])
            P.op(PE, lambda e, n=n: e.matmul(psb[6][:, 128:128 + n], lhsT=ones32[0:1, :], rhs=rrow[0:1, 0:n],
                                             start=True, stop=True), reads=["rrow", "ones32"], writes=[("ps", 6)])
            P.op(DVE, lambda e, n=n, col0=col0: e.tensor_copy(out=rstd_bc[:, col0:col0 + n], in_=psb[6][:, 128:128 + n]),
                 reads=[("ps", 6)], writes=["rstd_bc"])

    def proj_fm(wbuf, wkey, wc0, m, N, evac):
        pt, pkey = ps_gen()
        for c in range(NCH):
            P.op(PE, lambda e, c=c, pt=pt: e.matmul(pt[0:m, 0:N], lhsT=wbuf[:, c, wc0:wc0 + m], rhs=xT[:, c, 0:N],
                                                    start=(c == 0), stop=(c == NCH - 1)),
                 reads=[wkey, "xT"], writes=[pkey])
        evac(pt, pkey)

    def proj_tm(wbuf, wkey, wc0, m, col0, n, evac):
        pt, pkey = ps_gen()
        for c in range(NCH):
            P.op(PE, lambda e, c=c, pt=pt: e.matmul(pt[0:n, 0:m], lhsT=xT[:, c, col0:col0 + n],
                                                    rhs=wbuf[:, c, wc0:wc0 + m], start=(c == 0), stop=(c == NCH - 1)),
                 reads=[wkey, "xT"], writes=[pkey])
        evac(pt, pkey)

    def process_group(gi, tiles, N, gla_segs, sb_segs, out_rows, lite=False):
        A.reset(base_mark)
        ntile = len(tiles)
        QT4 = A([128, 4, 512], BF16)
        KT4 = A([128, 4, 512], BF16)
        m_vt = A.mark()
        Vt = A([128, 4, 512], BF16)
        m_st = A.mark()
        stage = [A([128, 512], F32) for _ in range(2)]
        gbS = A([128, 4, 512], BF16)
        tmpB = A([128, 512], F32)
        gaS = A([128, 4, 512], BF16)
        A32 = A([128, 4, 512], F32)
        alow = A([17, 512], F32)

        klen = max([(PAST + TS) if sg_[0] == "s" else (sg_[4] + sg_[3]) for sg_ in sb_segs])
        klen = ((klen + 127) // 128) * 128
        del wbs[2:]

        def sb_bufs():
            if lite:
                return None
            return {"kt": A([128, klen], BF16), "vt": A([128, klen // 128, 128], BF16), "e32": A([128, 512], F32R),
                    "e32b": A([128, 512], F32R), "w16": A([128, 512], BF16),
                    "carry": A([128, 512], F32R), "kc": A([128, 4, 128], F32), "tB": A([128, 512], F32)}
        SBB = [sb_bufs()]
        _save = A.mark()
        A.reset(m_st)
        if not lite:
            SBB[0]["arg"] = A([128, 512], F32)
        A.reset(m_vt)
        _x_arg = A([128, 512], F32)
        A.reset(_save)
        m_gla = A.mark()
        qT = A([128, 2, 512], BF16)
        kTg = A([128, 2, 512], BF16)
        k_tm = A([64, 8, 256], BF16)
        v_tm = A([64, 8, 512], BF16)
        g_tm = A([64, 8, 256], F32)
        sq16 = A([128, 4, 512], BF16)
        ecum = A([128, 2, 64], F32)
        encum = A([128, 2, 64], F32)
        qd = A([128, 2, 64], BF16)
        kd = A([128, 2, 64], BF16)
        erev = A([64, 256], F32)
        kd2 = A([64, 256], BF16)
        attT = A([64, 64], BF16)
        m_end = A.mark()
        A.reset(m_gla)
        SBB.append(sb_bufs())
        if not lite:
            SBB[1]["arg"] = _x_arg
        A.reset(max(m_end, A.mark()))
        if lite:
            k_tm2 = A([128, 4, 256], BF16)
            v_tm2 = A([128, 4, 512], BF16)
            g_tm2 = A([128, 4, 256], F32)
            erev2 = A([128, 256], F32)
            kd2b = A([128, 256], BF16)
            elast = A([128, 2, 1], F32)
        while A.top - A.mark() >= 16384 + 64 and len(wbs) < 4:
            wbs.append(A([128, NCH, 512], BF16))
        P.barrier(skip_dma=WBG2)

        norm_T(tiles, N, 0)

        P.op(DVE, lambda e: e.memset(alow[:], 1.0), writes=["alow"])
        wbuf, wkey = load_w(w_in_v, OFF_ALOW, 16)
        proj_fm(wbuf, wkey, 0, 16, N, lambda pt, pkey: P.op(
            DVE, lambda e: e.tensor_tensor(out=alow[0:16, 0:N], in0=pt[0:16, 0:N], in1=rstd_bc[0:16, 0:N], op=ALU.mult),
            reads=[pkey, "rstd_bc"], writes=["alow"]))

        all_chunks = []
        for (kind, seq, chunks) in gla_segs:
            for ch in chunks:
                all_chunks.append(ch)
        def do_quad_sb_lite(hg):
            h0 = 4 * hg
            wbuf, wkey = load_w(w_in_v, OFF_SK + 512 * hg, 512)
            for h4 in range(4):
                proj_fm(wbuf, wkey, h4 * 128, 128, N, lambda pt, pkey, h4=h4: P.op(
                    DVE, lambda e: e.tensor_tensor(out=KT4[:, h4, 0:N], in0=pt[:, 0:N], in1=rstd_bc[:, 0:N], op=ALU.mult),
                    reads=[pkey, "rstd_bc"], writes=["KT4"]))
                yield
            for (kind, seq, c0, NQ, pos0) in sb_segs:
                for h4 in range(4):
                    h = h0 + h4
                    P.op(SP, lambda e, h=h, h4=h4, c0=c0, NQ=NQ, pos0=pos0: e.dma_start(
                        out=kts.ap()[h * 128:(h + 1) * 128, pos0:pos0 + NQ], in_=KT4[:, h4, c0:c0 + NQ]),
                        reads=["KT4"], writes=[("kts", h)], dma="kts")
            def kv_rows(pt, pkey, ti, n, spec, is_v):
                dst, r0, p0, cntr = spec
                sg = stage[1 if is_v else 0]
                sk_ = ("stage", 1 if is_v else 0)
                P.op(ACT, lambda e: e.activation(out=sg[0:n, :], in_=pt[0:n, 0:512], func=AF.Copy,
                                                 scale=rstd_tm[0:n, ti:ti + 1]),
                     reads=[pkey, ("rstd", ti)], writes=[sk_])
                P.op(SP, lambda e: e.dma_start(out=dst.ap()[r0:r0 + cntr, 512 * hg:512 * hg + 512], in_=sg[p0:p0 + cntr, :]),
                     reads=[sk_], dma="stage%d" % (1 if is_v else 0))
                if is_v:
                    P.op(DVE, lambda e: e.tensor_copy(out=Vt[0:n, ti, :], in_=sg[0:n, :]), reads=[sk_], writes=[("Vt", ti)])

            for ti, (src, n, col0) in enumerate(tiles):
                spec = out_rows[ti][1]
                if spec[0] is not None:
                    proj_tm(wbuf, wkey, 0, 512, col0, n, lambda pt, pkey, ti=ti, n=n, spec=spec: kv_rows(
                        pt, pkey, ti, n, spec, False))
                    yield
            wbuf, wkey = load_w(w_in_v, OFF_SV + 512 * hg, 512)
            for ti, (src, n, col0) in enumerate(tiles):
                srow = out_rows[ti][3]
                spec = out_rows[ti][2]

                def evl(pt, pkey, ti=ti, n=n):
                    P.op(ACT, lambda e: e.activation(out=Vt[0:n, ti, :], in_=pt[0:n, 0:512], func=AF.Copy,
                                                     scale=rstd_tm[0:n, ti:ti + 1]),
                         reads=[pkey, ("rstd", ti)], writes=[("Vt", ti)])
                if spec[0] is not None:
                    proj_tm(wbuf, wkey, 0, 512, col0, n, lambda pt, pkey, ti=ti, n=n, spec=spec: kv_rows(
                        pt, pkey, ti, n, spec, True))
                    yield
                else:
                    proj_tm(wbuf, wkey, 0, 512, col0, n, evl)
                    yield
                for h4 in range(4):
                    h = h0 + h4
                    P.op(SP, lambda e, h=h, h4=h4, ti=ti, n=n, srow=srow: e.dma_start(
                        out=vts.ap()[h * TKS + srow:h * TKS + srow + n, :], in_=Vt[0:n, ti, h4 * 128:(h4 + 1) * 128]),
                        reads=[("Vt", ti)], writes=[("vts", h)], dma="vts")

        def do_quad_lite(hg):
            wbuf, wkey = load_w(w_in_v, OFF_GK + 256 * hg, 256)
            for ti, (src, n, col0) in enumerate(tiles):
                proj_tm(wbuf, wkey, 0, 256, col0, n, lambda pt, pkey, ti=ti, n=n: P.op(
                    ACT, lambda e: e.activation(out=k_tm2[0:n, ti, :], in_=pt[0:n, 0:256], func=AF.Copy,
                                                scale=rstd_tm[0:n, ti:ti + 1]),
                    reads=[pkey, ("rstd", ti)], writes=[("k_tm2", ti)]))
            wbuf, wkey = load_w(w_in_v, OFF_GV + 512 * hg, 512)
            for ti, (src, n, col0) in enumerate(tiles):
                proj_tm(wbuf, wkey, 0, 512, col0, n, lambda pt, pkey, ti=ti, n=n: P.op(
                    ACT, lambda e: e.activation(out=v_tm2[0:n, ti, :], in_=pt[0:n, 0:512], func=AF.Copy,
                                                scale=rstd_tm[0:n, ti:ti + 1]),
                    reads=[pkey, ("rstd", ti)], writes=[("v_tm2", ti)]))
            for ti, (src, n, col0) in enumerate(tiles):
                P.op(PE, lambda e, col0=col0, n=n: e.matmul(psb[6][0:n, 0:256], lhsT=alow[0:17, col0:col0 + n],
                                                            rhs=walp[0:17, 256 * hg:256 * hg + 256], start=True, stop=True),
                     reads=["alow", "walpw", "walpb"], writes=[("ps", 6)])
                P.op(ACT, lambda e, n=n: e.activation(out=erev2[0:n, :], in_=psb[6][0:n, 0:256], func=AF.Exp, scale=-1.0),
                     reads=[("ps", 6)], writes=["erev2"])
                P.op(ACT, lambda e, n=n: e.activation(out=erev2[0:n, :], in_=erev2[0:n, :], func=AF.Ln, bias=1.0),
                     reads=["erev2"], writes=["erev2"])
                P.op(DVE, lambda e, n=n, ti=ti: e.tensor_scalar(out=g_tm2[0:n, ti, :], in0=erev2[0:n, :],
                                                                scalar1=-1.0 / 16.0, scalar2=None, op0=ALU.mult),
                     reads=["erev2"], writes=[("g_tm2", ti)])

            def gen_chunks_lite():
                nch = len(all_chunks)
                for ci, (cc, ncn) in enumerate(all_chunks):
                    assert ncn == 64
                    ti, p0 = cc // 128, cc % 128
                    gch = g_tm2[p0:p0 + 64, ti, :]
                    ukey = "U32" if p0 == 0 else "U32hi"
                    for kc in range(2):
                        P.op(PE, lambda e, kc=kc, gch=gch, p0=p0: e.matmul(
                            psb[2][:, kc * 64:kc * 64 + 64], lhsT=gch[:, kc * 128:(kc + 1) * 128], rhs=U32[p0:p0 + 64, 0:64],
                            start=True, stop=True), reads=[("g_tm2", ti), ukey], writes=[("ps", 2)])
                    if p0 == 0:
                        P.op(PE, lambda e, gch=gch: e.matmul(psb[3][0:64, 0:256], lhsT=Ls32[0:64, 0:64], rhs=gch,
                                                            start=True, stop=True),
                             reads=[("g_tm2", ti), "Ls32"], writes=[("ps", 3)])
                    else:
                        P.op(PE, lambda e, gch=gch: e.matmul(psb[3][:, 0:256], lhsT=LsHi[64:128, :], rhs=gch,
                                                            start=True, stop=True),
                             reads=[("g_tm2", ti), "LsHi"], writes=[("ps", 3)])
                    plast = psb[2][:, 0:128].rearrange("p (k t) -> p k t", k=2)[:, :, 63:64]
                    P.op(ACT, lambda e, plast=plast: e.activation(out=elast[:, :, :], in_=plast, func=AF.Exp),
                         reads=[("ps", 2)], writes=["elast"])
                    P.op(ACT, lambda e, p0=p0: e.activation(out=erev2[p0:p0 + 64, :], in_=psb[3][p0:p0 + 64, 0:256], func=AF.Exp),
                         reads=[("ps", 3)], writes=["erev2"])
                    P.op(DVE, lambda e, p0=p0, ti=ti: e.tensor_tensor(out=kd2b[p0:p0 + 64, :], in0=k_tm2[p0:p0 + 64, ti, :],
                                                                      in1=erev2[p0:p0 + 64, :], op=ALU.mult),
                         reads=[("k_tm2", ti), "erev2"], writes=["kd2b"])
                    yield
                    for kc in range(2):
                        P.op(PE, lambda e, kc=kc, p0=p0, ti=ti: e.matmul(
                            psb[7][:, 0:512], lhsT=kd2b[p0:p0 + 64, kc * 128:(kc + 1) * 128],
                            rhs=v_tm2[p0:p0 + 64, ti, :], start=True, stop=True),
                            reads=["kd2b", ("v_tm2", ti)], writes=[("ps", 7)])
                        P.op(DVE, lambda e, kc=kc: e.scalar_tensor_tensor(
                            out=S[:, hg, kc, :], in0=S[:, hg, kc, :], scalar=elast[:, kc, 0:1], in1=psb[7][:, 0:512],
                            op0=ALU.mult, op1=ALU.add), reads=[("S", hg), "elast", ("ps", 7)], writes=[("S", hg)])
                    if ci == nch - 1:
                        P.op(ACT, lambda e: e.activation(out=Sb[:, hg, :, :], in_=S[:, hg, :, :], func=AF.Copy),
                             reads=[("S", hg)], writes=[("Sb", hg)])
                    yield

            _gens = [gen_chunks_lite(), do_quad_sb_lite(hg)]
            while _gens:
                for _g in list(_gens):
                    try:
                        next(_g)
                    except StopIteration:
                        _gens.remove(_g)

        def do_quad(hg):
            if not lite:
                wbuf, wkey = load_w(w_in_v, 256 * hg, 256)
            for kc in range(0 if lite else 2):
                proj_fm(wbuf, wkey, kc * 128, 128, N, lambda pt, pkey, kc=kc: P.op(
                    DVE, lambda e: e.scalar_tensor_tensor(out=qT[:, kc, 0:N], in0=pt[:, 0:N], scalar=DK ** -0.5,
                                                          in1=rstd_bc[:, 0:N], op0=ALU.mult, op1=ALU.mult),
                    reads=[pkey, "rstd_bc"], writes=["qT"]))
            wbuf, wkey = load_w(w_in_v, OFF_GK + 256 * hg, 256)
            for kc in range(0 if lite else 2):
                proj_fm(wbuf, wkey, kc * 128, 128, N, lambda pt, pkey, kc=kc: P.op(
                    DVE, lambda e: e.tensor_tensor(out=kTg[:, kc, 0:N], in0=pt[:, 0:N], in1=rstd_bc[:, 0:N], op=ALU.mult),
                    reads=[pkey, "rstd_bc"], writes=["kTg"]))
            rt_of = {}
            for ti, (src, n, col0) in enumerate(tiles):
                for ci, (cc, ncn) in enumerate(all_chunks):
                    if col0 <= cc < col0 + n:
                        rt_of[ci] = (ti, cc - col0)
            for ci, (cc, ncn) in enumerate(all_chunks):
                ti, p0 = rt_of[ci]
                pass
            for ci, (cc, ncn) in enumerate(all_chunks):
                if hg == 0:
                    P.op(PE, lambda e, cc=cc, ncn=ncn: e.transpose(out=psb[6][0:ncn, 256:257], in_=rstd_bc[0:1, cc:cc + ncn],
                                                                  identity=ident[0:1, 0:1]),
                         reads=["rstd_bc", "ident"], writes=[("ps", 6)])
                    P.op(DVE, lambda e, ci=ci, ncn=ncn: e.tensor_copy(out=ss[0:ncn, 0:1] if False else rs_tok[0:ncn, ci:ci + 1],
                                                                      in_=psb[6][0:ncn, 256:257]),
                         reads=[("ps", 6)], writes=[("rs_tok", ci)])
            for ci, (cc, ncn) in enumerate(all_chunks):
                proj_tm(wbuf, wkey, 0, 256, cc, ncn, lambda pt, pkey, ci=ci, ncn=ncn: P.op(
                    ACT, lambda e: e.activation(out=k_tm[0:ncn, ci, :], in_=pt[0:ncn, 0:256], func=AF.Copy,
                                                scale=rs_tok[0:ncn, ci:ci + 1]),
                    reads=[pkey, ("rs_tok", ci)], writes=[("k_tm", ci)]))
            wbuf, wkey = load_w(w_in_v, OFF_GV + 512 * hg, 512)
            for ci, (cc, ncn) in enumerate(all_chunks):
                proj_tm(wbuf, wkey, 0, 512, cc, ncn, lambda pt, pkey, ci=ci, ncn=ncn: P.op(
                    ACT, lambda e: e.activation(out=v_tm[0:ncn, ci, :], in_=pt[0:ncn, 0:512], func=AF.Copy,
                                                scale=rs_tok[0:ncn, ci:ci + 1]),
                    reads=[pkey, ("rs_tok", ci)], writes=[("v_tm", ci)]))
            if not lite:
                wbuf, wkey = load_w(w_in_v, OFF_GA + 512 * hg, 512)
            for dc in range(0 if lite else 4):
                def ev(pt, pkey, dc=dc):
                    P.op(DVE, lambda e: e.tensor_tensor(out=tmpB[:, 0:N], in0=pt[:, 0:N], in1=rstd_bc[:, 0:N], op=ALU.mult),
                         reads=[pkey, "rstd_bc"], writes=["tmpB"])
                    P.op(ACT, lambda e: e.activation(out=gaS[:, dc, 0:N], in_=tmpB[:, 0:N], func=AF.Sigmoid),
                         reads=["tmpB"], writes=["gaS"])
                proj_fm(wbuf, wkey, dc * 128, 128, N, ev)
            for ci, (cc, ncn) in enumerate(all_chunks):
                P.op(PE, lambda e, cc=cc, ncn=ncn: e.matmul(psb[6][0:ncn, 0:256], lhsT=alow[0:17, cc:cc + ncn],
                                                            rhs=walp[0:17, 256 * hg:256 * hg + 256], start=True, stop=True),
                     reads=["alow", "walpw", "walpb"], writes=[("ps", 6)])
                P.op(ACT, lambda e, ncn=ncn: e.activation(out=erev[0:ncn, :], in_=psb[6][0:ncn, 0:256], func=AF.Exp, scale=-1.0),
                     reads=[("ps", 6)], writes=["erev"])
                P.op(ACT, lambda e, ncn=ncn: e.activation(out=erev[0:ncn, :], in_=erev[0:ncn, :], func=AF.Ln, bias=1.0),
                     reads=["erev"], writes=["erev"])
                P.op(DVE, lambda e, ncn=ncn, ci=ci: e.tensor_scalar(out=g_tm[0:ncn, ci, :], in0=erev[0:ncn, :],
                                                                    scalar1=-1.0 / 16.0, scalar2=None, op0=ALU.mult),
                     reads=["erev"], writes=[("g_tm", ci)])
            def gen_chunks():
                ci = 0
                for (kind, seq, chunks) in gla_segs:
                    if kind == "s":
                        r0 = (seq * H_GLA + hg) * DK
                        P.op(SP, lambda e, r0=r0: e.dma_start(out=S[:, hg, :, :],
                                                              in_=st.ap()[r0:r0 + DK, :].rearrange("(k p) v -> p k v", p=128)),
                             writes=[("S", hg)], dma="S%d" % hg)
                        P.op(ACT, lambda e: e.activation(out=Sb[:, hg, :, :], in_=S[:, hg, :, :], func=AF.Copy),
                             reads=[("S", hg)], writes=[("Sb", hg)])
                    for (cc, ncn) in chunks:
                        gch = g_tm[0:ncn, ci, :]
                        for kc in range(2):
                            P.op(PE, lambda e, kc=kc, ncn=ncn, gch=gch: e.matmul(
                                psb[2][:, kc * 64:kc * 64 + ncn], lhsT=gch[:, kc * 128:(kc + 1) * 128], rhs=U32[0:ncn, 0:ncn],
                                start=True, stop=True), reads=[("g_tm", ci), "U32"], writes=[("ps", 2)])
                        P.op(PE, lambda e, ncn=ncn, gch=gch: e.matmul(psb[3][0:ncn, 0:256], lhsT=Ls32[0:ncn, 0:ncn], rhs=gch,
                                                                      start=True, stop=True),
                             reads=[("g_tm", ci), "Ls32"], writes=[("ps", 3)])
                        pc3 = psb[2][:, 0:128].rearrange("p (k t) -> p k t", k=2)[:, :, 0:ncn]
                        P.op(ACT, lambda e, ncn=ncn, pc3=pc3: e.activation(out=ecum[:, :, 0:ncn], in_=pc3, func=AF.Exp),
                             reads=[("ps", 2)], writes=["ecum"])
                        if not lite:
                            P.op(ACT, lambda e, ncn=ncn, pc3=pc3: e.activation(out=encum[:, :, 0:ncn], in_=pc3, func=AF.Exp, scale=-1.0),
                                 reads=[("ps", 2)], writes=["encum"])
                            P.op(DVE, lambda e, ncn=ncn, cc=cc: e.tensor_tensor(out=qd[:, :, 0:ncn], in0=qT[:, :, cc:cc + ncn],
                                                                                in1=ecum[:, :, 0:ncn], op=ALU.mult),
                                 reads=["qT", "ecum"], writes=["qd"])
                            P.op(DVE, lambda e, ncn=ncn, cc=cc: e.tensor_tensor(out=kd[:, :, 0:ncn], in0=kTg[:, :, cc:cc + ncn],
                                                                                in1=encum[:, :, 0:ncn], op=ALU.mult),
                                 reads=["kTg", "encum"], writes=["kd"])
                        P.op(ACT, lambda e, ncn=ncn: e.activation(out=erev[0:ncn, :], in_=psb[3][0:ncn, 0:256], func=AF.Exp),
                             reads=[("ps", 3)], writes=["erev"])
                        P.op(DVE, lambda e, ncn=ncn, ci=ci: e.tensor_tensor(out=kd2[0:ncn, :], in0=k_tm[0:ncn, ci, :],
                                                                            in1=erev[0:ncn, :], op=ALU.mult),
                             reads=[("k_tm", ci), "erev"], writes=["kd2"])
                        yield
                        for kc in range(0 if lite else 2):
                            P.op(PE, lambda e, kc=kc, ncn=ncn: e.matmul(psb[3][0:ncn, 256:256 + ncn], lhsT=kd[:, kc, 0:ncn],
                                                                        rhs=qd[:, kc, 0:ncn], start=(kc == 0), stop=(kc == 1)),
                                 reads=["kd", "qd"], writes=[("ps", 3)])
                        if not lite:
                            P.op(DVE, lambda e, ncn=ncn: e.tensor_tensor(out=attT[0:ncn, 0:ncn], in0=psb[3][0:ncn, 256:256 + ncn],
                                                                         in1=U32[0:ncn, 0:ncn], op=ALU.mult),
                                 reads=[("ps", 3), "U32"], writes=["attT"])
                        for dc in range(0 if lite else 4):
                            P.op(PE, lambda e, dc=dc, ncn=ncn, ci=ci: e.matmul(
                                psb[2][:, 128 + dc * 64:128 + dc * 64 + ncn], lhsT=v_tm[0:ncn, ci, dc * 128:(dc + 1) * 128],
                                rhs=attT[0:ncn, 0:ncn], start=True, stop=False),
                                reads=[("v_tm", ci), "attT"], writes=[("ps", 2)])
                            for kc in range(2):
                                P.op(PE, lambda e, dc=dc, kc=kc, ncn=ncn: e.matmul(
                                    psb[2][:, 128 + dc * 64:128 + dc * 64 + ncn], lhsT=Sb[:, hg, kc, dc * 128:(dc + 1) * 128],
                                    rhs=qd[:, kc, 0:ncn], start=False, stop=(kc == 1)),
                                    reads=[("Sb", hg), "qd"], writes=[("ps", 2)])
                        po3 = psb[2][:, 128:384].rearrange("p (d t) -> p d t", d=4)[:, :, 0:ncn]
                        if not lite:
                            P.op(ACT, lambda e, po3=po3, cc=cc, ncn=ncn: e.activation(out=A32[:, :, cc:cc + ncn], in_=po3, func=AF.Copy),
                                 reads=[("ps", 2)], writes=["A32"])
                        for kc in range(2):
                            P.op(PE, lambda e, kc=kc, ncn=ncn, ci=ci: e.matmul(
                                psb[3][:, 0:512] if False else psb[7][:, 0:512], lhsT=kd2[0:ncn, kc * 128:(kc + 1) * 128],
                                rhs=v_tm[0:ncn, ci, :], start=True, stop=True),
                                reads=["kd2", ("v_tm", ci)], writes=[("ps", 7)])
                            P.op(DVE, lambda e, kc=kc, ncn=ncn: e.scalar_tensor_tensor(
                                out=S[:, hg, kc, :], in0=S[:, hg, kc, :], scalar=ecum[:, kc, ncn - 1:ncn], in1=psb[7][:, 0:512],
                                op0=ALU.mult, op1=ALU.add), reads=[("S", hg), "ecum", ("ps", 7)], writes=[("S", hg)])
                        if (not lite) or ci == len(all_chunks) - 1:
                            P.op(ACT, lambda e: e.activation(out=Sb[:, hg, :, :], in_=S[:, hg, :, :], func=AF.Copy),
                                 reads=[("S", hg)], writes=[("Sb", hg)])
                        ci += 1
                        yield
                    if kind == "s":
                        r0 = (seq * H_GLA + hg) * DK
                        P.op(SP, lambda e, r0=r0: e.dma_start(out=o_gs.ap()[r0:r0 + DK, :].rearrange("(k p) v -> p k v", p=128),
                                                              in_=S[:, hg, :, :]), reads=[("S", hg)], dma="S%d" % hg)
                    elif kind == "p_last":
                        r0 = hg * DK
                        P.op(SP, lambda e, r0=r0: e.dma_start(out=o_gp.ap()[r0:r0 + DK, :].rearrange("(k p) v -> p k v", p=128),
                                                              in_=S[:, hg, :, :]), reads=[("S", hg)], dma="S%d" % hg)
            h0 = 4 * hg

            def gen_sbproj():
                wbuf, wkey = load_w(w_in_v, OFF_SQ + 512 * hg, 512)
                for h4 in range(4):
                    proj_fm(wbuf, wkey, h4 * 128, 128, N, lambda pt, pkey, h4=h4: P.op(
                        DVE, lambda e: e.scalar_tensor_tensor(out=QT4[:, h4, 0:N], in0=pt[:, 0:N], scalar=128 ** -0.5,
                                                              in1=rstd_bc[:, 0:N], op0=ALU.mult, op1=ALU.mult),
                        reads=[pkey, "rstd_bc"], writes=["QT4"]))
                    yield
                wbuf, wkey = load_w(w_in_v, OFF_SK + 512 * hg, 512)
                for h4 in range(4):
                    proj_fm(wbuf, wkey, h4 * 128, 128, N, lambda pt, pkey, h4=h4: P.op(
                        DVE, lambda e: e.tensor_tensor(out=KT4[:, h4, 0:N], in0=pt[:, 0:N], in1=rstd_bc[:, 0:N], op=ALU.mult),
                        reads=[pkey, "rstd_bc"], writes=["KT4"]))
                    yield
                kv_dst = []
                for (kind, seq, c0, NQ, pos0) in sb_segs:
                    scol = pos0 if kind != "s" else TKP + seq * TS
                    kv_dst.append((c0, NQ, scol))
                    for h4 in range(4):
                        h = h0 + h4
                        P.op(SP, lambda e, h=h, h4=h4, c0=c0, NQ=NQ, scol=scol: e.dma_start(
                            out=kts.ap()[h * 128:(h + 1) * 128, scol:scol + NQ], in_=KT4[:, h4, c0:c0 + NQ]),
                            reads=["KT4"], writes=[("kts", h)], dma="kts")
                sidx = [0]

                def tm_out(pt, pkey, ti, n, spec, is_v):
                    dst, r0, p0, cntr = spec
                    sgi = sidx[0] % 2
                    sidx[0] += 1
                    sg = stage[sgi]
                    P.op(ACT, lambda e: e.activation(out=sg[0:n, :], in_=pt[0:n, 0:512], func=AF.Copy,
                                                     scale=rstd_tm[0:n, ti:ti + 1]),
                         reads=[pkey, ("rstd", ti)], writes=[("stage", sgi)])
                    if dst is not None:
                        P.op(SP, lambda e: e.dma_start(out=dst.ap()[r0:r0 + cntr, 512 * hg:512 * hg + 512],
                                                       in_=sg[p0:p0 + cntr, :]),
                             reads=[("stage", sgi)], dma="stage%d" % sgi)
                    if is_v:
                        P.op(DVE, lambda e: e.tensor_copy(out=Vt[0:n, ti, :], in_=sg[0:n, :]),
                             reads=[("stage", sgi)], writes=[("Vt", ti)])

                for ti, (src, n, col0) in enumerate(tiles):
                    spec = out_rows[ti][1]
                    proj_tm(wbuf, wkey, 0, 512, col0, n, lambda pt, pkey, ti=ti, n=n, spec=spec: tm_out(
                        pt, pkey, ti, n, spec, False))
                    yield
                wbuf, wkey = load_w(w_in_v, OFF_SV + 512 * hg, 512)
                for ti, (src, n, col0) in enumerate(tiles):
                    spec = out_rows[ti][2]
                    proj_tm(wbuf, wkey, 0, 512, col0, n, lambda pt, pkey, ti=ti, n=n, spec=spec: tm_out(
                        pt, pkey, ti, n, spec, True))
                    yield
                    srow = out_rows[ti][3]
                    for h4 in range(4):
                        h = h0 + h4
                        P.op(SP, lambda e, h=h, h4=h4, ti=ti, n=n, srow=srow: e.dma_start(
                            out=vts.ap()[h * TKS + srow:h * TKS + srow + n, :], in_=Vt[0:n, ti, h4 * 128:(h4 + 1) * 128]),
                            reads=[("Vt", ti)], writes=[("vts", h)], dma="vts")
                wbuf, wkey = load_w(w_in_v, OFF_GB + 512 * hg, 512)
                for h4 in range(4):
                    def evb(pt, pkey, h4=h4):
                        P.op(DVE, lambda e: e.tensor_tensor(out=tmpB[:, 0:N], in0=pt[:, 0:N], in1=rstd_bc[:, 0:N], op=ALU.mult),
                             reads=[pkey, "rstd_bc"], writes=["tmpB"])
                        P.op(ACT, lambda e: e.activation(out=gbS[:, h4, 0:N], in_=tmpB[:, 0:N], func=AF.Sigmoid),
                             reads=["tmpB"], writes=["gbS"])
                    proj_fm(wbuf, wkey, h4 * 128, 128, N, evb)
                    yield

            _gens = [gen_chunks(), do_quad_sb_lite(hg) if lite else gen_sbproj()]
            while _gens:
                for _g in list(_gens):
                    try:
                        next(_g)
                    except StopIteration:
                        _gens.remove(_g)
            if lite:
                return
            P.op(ACT, lambda e: e.activation(out=sq16[:, :, 0:N], in_=A32[:, :, 0:N], func=AF.Square),
                 reads=["A32"], writes=["sq16"])
            for dc in range(4):
                P.op(PE, lambda e, dc=dc: e.matmul(psb[6][:, 0:N], lhsT=ones16[:, :], rhs=sq16[:, dc, 0:N],
                                                   start=(dc == 0), stop=(dc == 3)),
                     reads=["sq16", "ones16"], writes=[("ps", 6)])
            P.op(DVE, lambda e: e.tensor_scalar(out=tmpB[:, 0:N], in0=psb[6][:, 0:N], scalar1=1.0 / DV, scalar2=EPS,
                                                op0=ALU.mult, op1=ALU.add), reads=[("ps", 6)], writes=["tmpB"])
            P.op(ACT, lambda e: e.activation(out=tmpB[:, 0:N], in_=tmpB[:, 0:N], func=AF.Sqrt),
                 reads=["tmpB"], writes=["tmpB"])
            P.op(DVE, lambda e: e.reciprocal(out=tmpB[:, 0:N], in_=tmpB[:, 0:N]), reads=["tmpB"], writes=["tmpB"])
            for dc in range(4):
                gi_ = 48 + hg * 4 + dc
                P.op(DVE, lambda e, dc=dc, gi_=gi_: e.scalar_tensor_tensor(
                    out=A32[:, dc, 0:N], in0=A32[:, dc, 0:N], scalar=gT[:, gi_:gi_ + 1], in1=tmpB[:, 0:N],
                    op0=ALU.mult, op1=ALU.mult), reads=["A32", "gT", "tmpB"], writes=["A32"])
            P.op(DVE, lambda e: e.tensor_tensor(out=A32[:, :, 0:N], in0=A32[:, :, 0:N], in1=gaS[:, :, 0:N], op=ALU.mult),
                 reads=["A32", "gaS"], writes=["A32"])

            P.barrier(skip_dma=WBG4)

            def sb_job(sl, h4, seg):
                (kind, seq, c0, NQ, pos0) = seg
                h = h0 + h4
                B_ = SBB[sl]
                kt_, vt_, w16_, carry_, kc_, tB_ = (B_["kt"], B_["vt"], B_["w16"], B_["carry"], B_["kc"], B_["tB"])
                e32s = (B_["e32"], B_["e32b"])
                bz, bz2, bk, bo = (2, 3, 4, 5) if sl == 0 else (6, 7, 0, 1)
                K = lambda nm: (nm, sl)
                if kind == "s":
                    for half in range(NPT // 4 if NPT >= 4 else 1):
                        nt = min(4, NPT)
                        rbase = seq * PAST + half * 512
                        P.op(SP, lambda e, rbase=rbase, nt=nt: e.dma_start(
                            out=kc_[:, 0:nt, :],
                            in_=ck.ap()[rbase:rbase + nt * 128, h * 128:(h + 1) * 128].rearrange("(j p) d -> p j d", p=128)),
                            writes=[K("kc")], dma="kc%d" % sl)
                        for j in range(nt):
                            P.op(PE, lambda e, j=j: e.transpose(out=psb[bk][:, j * 128:(j + 1) * 128], in_=kc_[:, j, :],
                                                                identity=ident[:, :]),
                                 reads=[K("kc"), "ident"], writes=[("ps", bk)])
                        P.op(ACT, lambda e, half=half, nt=nt: e.activation(
                            out=kt_[:, half * 512:half * 512 + nt * 128], in_=psb[bk][:, 0:nt * 128], func=AF.Copy),
                            reads=[("ps", bk)], writes=[K("kt")])
                    rb = seq * PAST
                    P.op(POOL, lambda e, rb=rb: e.dma_start(
                        out=vt_[:, 0:NPT, :],
                        in_=cv.ap()[rb:rb + PAST, h * 128:(h + 1) * 128].rearrange("(j p) d -> p j d", p=128)),
                        writes=[K("vt")], dma="vt%d" % sl)
                    scol = TKP + seq * TS
                    P.op(SP, lambda e, scol=scol: e.dma_start(out=kt_[:, PAST:PAST + TS],
                                                              in_=kts.ap()[h * 128:(h + 1) * 128, scol:scol + TS]),
                         reads=[("kts", h)], writes=[K("kt")], dma="kt%d" % sl)
                    P.op(SP, lambda e, scol=scol: e.dma_start(out=vt_[0:TS, NPT, :],
                                                              in_=vts.ap()[h * TKS + scol:h * TKS + scol + TS, :]),
                         reads=[("vts", h)], writes=[K("vt")], dma="vt%d" % sl)
                    ktiles = [(128, None)] * NPT + [(TS, 0)]
                else:
                    kend = pos0 + NQ
                    nfull = kend // 128
                    rem = kend - nfull * 128
                    P.op(SP, lambda e, kend=kend: e.dma_start(out=kt_[:, 0:kend], in_=kts.ap()[h * 128:(h + 1) * 128, 0:kend]),
                         reads=[("kts", h)], writes=[K("kt")], dma="kt%d" % sl)
                    if nfull > 0:
                        P.op(SP, lambda e, nfull=nfull: e.dma_start(
                            out=vt_[:, 0:nfull, :],
                            in_=vts.ap()[h * TKS:h * TKS + nfull * 128, :].rearrange("(j p) d -> p j d", p=128)),
                            reads=[("vts", h)], writes=[K("vt")], dma="vt%d" % sl)
                    if rem > 0:
                        P.op(SP, lambda e, nfull=nfull, rem=rem: e.dma_start(
                            out=vt_[0:rem, nfull, :], in_=vts.ap()[h * TKS + nfull * 128:h * TKS + nfull * 128 + rem, :]),
                            reads=[("vts", h)], writes=[K("vt")], dma="vt%d" % sl)
                    ktiles = []
                    for j in range(nfull + (1 if rem else 0)):
                        kn = 128 if j < nfull else rem
                        d = j * 128 - pos0
                        if d + 127 < 0:
                            ktiles.append((kn, None))
                        else:
                            assert d % 128 == 0 and 0 <= d // 128 < 4
                            ktiles.append((kn, d // 128))
                QTh = QT4[:, h4, c0:c0 + NQ]
                P.op(DVE, lambda e: e.tensor_scalar(out=carry_[:, 0:NQ], in0=rstd_bc[:, 0:NQ], scalar1=0.0, scalar2=None,
                                                    op0=ALU.mult), reads=["rstd_bc"], writes=[K("carry")])
                yield
                nkt = len(ktiles)
                order = list(reversed(range(nkt)))
                bzs = (bz, bz2)
                arg_ = B_["arg"]

                def S1(t):
                    j = order[t]
                    kn, dg = ktiles[j]
                    p = t % 2
                    zb, eb = bzs[p], e32s[p]
                    ktj = kt_[:, j * 128:j * 128 + kn]
                    P.op(PE, lambda e: e.matmul(psb[zb][0:kn, 0:NQ], lhsT=ktj, rhs=QTh, start=True, stop=False,
                                                skip_group_check=True),
                         reads=[K("kt"), "QT4"], writes=[("ps", zb)])
                    P.op(ACT, lambda e: e.activation(out=tB_[0:kn, 0:NQ], in_=psb[zb][0:kn, 0:NQ], func=AF.Exp),
                         reads=[("ps", zb)], writes=[K("tB")])
                    P.op(ACT, lambda e: e.activation(out=eb[0:kn, 0:NQ], in_=tB_[0:kn, 0:NQ], func=AF.Ln, bias=1.0),
                         reads=[K("tB")], writes=[K(("e32", p))])
                    if dg is not None:
                        ebf = eb.bitcast(F32)
                        P.op(DVE, lambda e: e.tensor_tensor(
                            out=eb[0:kn, 0:NQ], in0=ebf[0:kn, 0:NQ], in1=masks[0:kn, dg, 0:NQ], op=ALU.mult),
                            reads=[K(("e32", p)), "masks"], writes=[K(("e32", p))])

                def S2(t):
                    j = order[t]
                    kn, dg = ktiles[j]
                    p = t % 2
                    zb, eb = bzs[p], e32s[p]
                    P.op(PE, lambda e: e.matmul(psb[zb][:, 0:NQ], lhsT=negTr[0:kn, :], rhs=eb[0:kn, 0:NQ],
                                                start=False, stop=(t == 0), skip_group_check=True),
                         reads=["negTr", K(("e32", p))], writes=[("ps", zb)])
                    if t > 0:
                        P.op(PE, lambda e: e.matmul(psb[zb][:, 0:NQ], lhsT=nonesr[:, :], rhs=carry_[:, 0:NQ],
                                                    start=False, stop=True, skip_group_check=True),
                             reads=["nonesr", K("carry")], writes=[("ps", zb)])
                    P.op(ACT, lambda e: e.activation(out=w16_[0:kn, 0:NQ], in_=psb[zb][0:kn, 0:NQ], func=AF.Exp),
                         reads=[("ps", zb)], writes=[K("w16")])
                    if dg is not None:
                        P.op(DVE, lambda e: e.tensor_tensor(
                            out=w16_[0:kn, 0:NQ], in0=w16_[0:kn, 0:NQ], in1=masks[0:kn, dg, 0:NQ], op=ALU.mult),
                            reads=[K("w16"), "masks"], writes=[K("w16")])
                    if j > 0:
                        cf, ef = carry_.bitcast(F32), eb.bitcast(F32)
                        P.op(DVE, lambda e: e.tensor_tensor(out=carry_[0:kn, 0:NQ], in0=cf[0:kn, 0:NQ],
                                                            in1=ef[0:kn, 0:NQ], op=ALU.add),
                             reads=[K("carry"), K(("e32", p))], writes=[K("carry")])

                def S3(t):
                    j = order[t]
                    kn, dg = ktiles[j]
                    P.op(PE, lambda e: e.matmul(psb[bo][:, 0:NQ], lhsT=vt_[0:kn, j, :], rhs=w16_[0:kn, 0:NQ],
                                                start=(t == 0), stop=(t == nkt - 1)),
                         reads=[K("vt"), K("w16")], writes=[("ps", bo)])

                for k in range(nkt + 2):
                    if 0 <= k - 2 < nkt:
                        S3(k - 2)
                        yield
                    if 0 <= k - 1 < nkt:
                        S2(k - 1)
                        yield
                    if k < nkt:
                        S1(k)
                        yield

                P.op(DVE, lambda e: e.tensor_tensor(out=tB_[:, 0:NQ], in0=psb[bo][:, 0:NQ],
                                                    in1=gbS[:, h4, c0:c0 + NQ], op=ALU.mult),
                     reads=[("ps", bo), "gbS"], writes=[K("tB")])
                P.op(DVE, lambda e: e.tensor_tensor(
                    out=mixT[:, h, c0:c0 + NQ], in0=tB_[:, 0:NQ], in1=A32[:, h4, c0:c0 + NQ], op=ALU.add),
                    reads=[K("tB"), "A32"], writes=["mixT"])

            jobs = [(h4, seg) for h4 in range(4) for seg in sb_segs]
            active = [None, None]
            while jobs or any(a is not None for a in active):
                for sl in range(2):
                    if active[sl] is None and jobs:
                        h4_, seg_ = jobs.pop(0)
                        active[sl] = sb_job(sl, h4_, seg_)
                    if active[sl] is not None:
                        try:
                            next(active[sl])
                        except StopIteration:
                            active[sl] = None
            P.barrier(skip_dma=WBG4)


        for hg_ in range(H_GLA):
            if lite:
                do_quad_lite(hg_)
            else:
                do_quad(hg_)
        if lite:
            return

        A.reset(base_mark)
        xacc = A([128, 4, D], F32)
        hT = A([128, 32, 512], BF16)
        gf_bc = A([128, D], F32)
        yo = A([128, D], F32)
        del wbs[2:]
        while A.top - A.mark() >= 16384 + 64 and len(wbs) < 4:
            wbs.append(A([128, NCH, 512], BF16))
        P.barrier(skip_dma=WBG2)
        P.op(SP, lambda e: e.dma_start(out=yo[0:1, :], in_=gf_row.ap()), writes=["yo"], dma="yo")
        for q in range(4):
            P.op(PE, lambda e, q=q: e.matmul(psb[6][:, 0:512], lhsT=ones32[0:1, :], rhs=yo[0:1, q * 512:(q + 1) * 512],
                                             start=True, stop=True), reads=["yo", "ones32"], writes=[("ps", 6)])
            P.op(DVE, lambda e, q=q: e.tensor_copy(out=gf_bc[:, q * 512:(q + 1) * 512], in_=psb[6][:, 0:512]),
                 reads=[("ps", 6)], writes=["gf_bc"])
        for cb in range(4):
            wbuf, wkey = load_w(w_out_v, cb * 512, 512)
            for ti, (src, n, col0) in enumerate(tiles):
                if cb == 0:
                    pass
                P.op(SP, lambda e, src=src, n=n, cb=cb: e.dma_start(out=xin[0:n, 0:512], in_=src[:, cb * 512:(cb + 1) * 512]),
                     writes=["xin"], dma="xin")
                pt, pkey = ps_gen()
                for c in range(NCH):
                    P.op(PE, lambda e, c=c, pt=pt, n=n, col0=col0, wbuf=wbuf: e.matmul(
                        pt[0:n, 0:512], lhsT=mixT[:, c, col0:col0 + n], rhs=wbuf[:, c, 0:512],
                        start=(c == 0), stop=(c == NCH - 1)), reads=[wkey, "mixT"], writes=[pkey])
                P.op(DVE, lambda e, pt=pt, n=n, ti=ti, cb=cb: e.tensor_tensor(
                    out=xacc[0:n, ti, cb * 512:(cb + 1) * 512], in0=pt[0:n, 0:512], in1=xin[0:n, 0:512], op=ALU.add),
                    reads=[pkey, "xin"], writes=[("xacc", ti)])
        norm_T(tiles, N, 16, src_is_acc=xacc)
        for hh in range(2):
            for hb in range(8):
                wbuf, wkey = load_w(w_up_v, hh * 4096 + hb * 512, 512)
                for hc in range(4):
                    def evm(pt, pkey, idx=hb * 4 + hc):
                        P.op(DVE, lambda e: e.tensor_tensor(out=yo[:, 0:N], in0=pt[:, 0:N], in1=rstd_bc[:, 0:N], op=ALU.mult),
                             reads=[pkey, "rstd_bc"], writes=["yo"])
                        P.op(DVE, lambda e: e.scalar_tensor_tensor(out=hT[:, idx, 0:N], in0=yo[:, 0:N], scalar=0.0,
                                                                   in1=yo[:, 0:N], op0=ALU.max, op1=ALU.mult),
                             reads=["yo"], writes=["hT"])
                    proj_fm(wbuf, wkey, hc * 128, 128, N, evm)
            for cb in range(8):
                i = cnt["wb"] % len(wbs)
                cnt["wb"] += 1
                buf = wbs[i]
                wkey = ("wb", i)
                wdv = w_down.ap()[hh * 4096:(hh + 1) * 4096, cb * 256:(cb + 1) * 256].rearrange("(c p) m -> p c m", p=128)
                bview = buf[:].rearrange("p c m -> p (c m)")[:, 0:32 * 256].rearrange("p (c m) -> p c m", m=256)
                P.op(POOL, lambda e, bview=bview, wdv=wdv: e.dma_start(out=bview, in_=wdv), writes=[wkey], dma="wb%d" % i,
                     nobar=(i < 2))
                for ti, (src, n, col0) in enumerate(tiles):
                    pt, pkey = ps_gen()
                    for c in range(32):
                        P.op(PE, lambda e, c=c, pt=pt, n=n, col0=col0, bview=bview: e.matmul(
                            pt[0:n, 0:256], lhsT=hT[:, c, col0:col0 + n], rhs=bview[:, c, :],
                            start=(c == 0), stop=(c == 31)), reads=[wkey, "hT"], writes=[pkey])
                    P.op(DVE, lambda e, pt=pt, n=n, ti=ti, cb=cb: e.tensor_tensor(
                        out=xacc[0:n, ti, cb * 256:(cb + 1) * 256], in0=pt[0:n, 0:256],
                        in1=xacc[0:n, ti, cb * 256:(cb + 1) * 256], op=ALU.add),
                        reads=[pkey, ("xacc", ti)], writes=[("xacc", ti)])
        for ti, (src, n, col0) in enumerate(tiles):
            P.op(ACT, lambda e, n=n, ti=ti: e.activation(out=junk[0:n, 0:D], in_=xacc[0:n, ti, :], func=AF.Square,
                                                         accum_out=ss[0:n, ti:ti + 1]),
                 reads=[("xacc", ti)], writes=["mixT", ("ss", ti)])
            P.op(DVE, lambda e, n=n, ti=ti: e.tensor_scalar(out=rstd_tm[0:n, ti:ti + 1], in0=ss[0:n, ti:ti + 1],
                                                            scalar1=1.0 / D, scalar2=EPS, op0=ALU.mult, op1=ALU.add),
                 reads=[("ss", ti)], writes=[("rstd", ti)])
            P.op(ACT, lambda e, n=n, ti=ti: e.activation(out=rstd_tm[0:n, ti:ti + 1], in_=rstd_tm[0:n, ti:ti + 1],
                                                         func=AF.Sqrt), reads=[("rstd", ti)], writes=[("rstd", ti)])
            P.op(DVE, lambda e, n=n, ti=ti: e.reciprocal(out=rstd_tm[0:n, ti:ti + 1], in_=rstd_tm[0:n, ti:ti + 1]),
                 reads=[("rstd", ti)], writes=[("rstd", ti)])
            P.op(DVE, lambda e, n=n, ti=ti: e.scalar_tensor_tensor(out=yo[0:n, :], in0=xacc[0:n, ti, :],
                                                                   scalar=rstd_tm[0:n, ti:ti + 1], in1=gf_bc[0:n, :],
                                                                   op0=ALU.mult, op1=ALU.mult),
                 reads=[("xacc", ti), ("rstd", ti), "gf_bc"], writes=["yo"])
            dst, r0, p0, cnt_rows = out_rows[ti][0]
            if cnt_rows > 0:
                P.op(SP, lambda e, dst=dst, r0=r0, p0=p0, cnt_rows=cnt_rows: e.dma_start(
                    out=dst.ap()[r0:r0 + cnt_rows, :], in_=yo[p0:p0 + cnt_rows, :]), reads=["yo"], dma="yo")

    A.reset(base_mark)
    rs_tok = A([64, 16], F32)
    base_mark = A.mark()

    NONE4 = (None, 0, 0, 0)
    meta_spec = lambda dst: (dst, 0, 128 - N_META, N_META)
    process_group(-1, [(xp.ap()[0:128, :], 128, 0)], 128, [("p", 0, [(0, 64), (64, 64)])], [("p", 0, 0, 128, 0)],
                  [(NONE4, meta_spec(o_kp) if NGP == 0 else NONE4, meta_spec(o_vp) if NGP == 0 else NONE4, 0)], lite=True)
    for g in range(NG_ALL):
        tiles, out_rows = [], []
        lite = g < NGP
        base = 128 + g * 512
        for t in range(4):
            tok0 = base + t * 128
            tiles.append((xp.ap()[tok0:tok0 + 128, :], 128, t * 128))
            if lite:
                last16 = (g == NGP - 1 and t == 3)
                out_rows.append((NONE4, meta_spec(o_kp) if last16 else NONE4, meta_spec(o_vp) if last16 else NONE4, tok0))
            else:
                loc = (g - NGP) * 512 + t * 128
                out_rows.append(((o_yp, loc, 0, 128), (o_kp, N_META + loc, 0, 128), (o_vp, N_META + loc, 0, 128), tok0))
        chunks = [(c * 64, 64) for c in range(8)]
        kind = "p_last" if g == NG_ALL - 1 else "p"
        process_group(g, tiles, 512, [(kind, 0, chunks)], [("p", 0, 0, 512, base)], out_rows, lite=lite)
    tiles, out_rows, gla_segs, sb_segs = [], [], [], []
    col = 0
    for sq_ in range(NS):
        tiles.append((xsm.ap()[sq_ * TS:(sq_ + 1) * TS, :], TS, col))
        out_rows.append(((o_ys, sq_ * TS, 0, TS), (o_ks, sq_ * TS, 0, TS), (o_vs, sq_ * TS, 0, TS), TKP + sq_ * TS))
        gla_segs.append(("s", sq_, [(col, TS)]))
        sb_segs.append(("s", sq_, col, TS, PAST))
        col += TS
    process_group(NG_ALL, tiles, col, gla_segs, sb_segs, out_rows)

    P.emit()
    es.close()
    return nc


_NC_CACHE = {}


def _run(inputs, SEQ, PAST, n_cores=8):
    key = (SEQ, PAST)
    if key not in _NC_CACHE:
        _NC_CACHE[key] = build_nc(SEQ, PAST)
    nc = _NC_CACHE[key]
    f = lambda a: np.ascontiguousarray(np.asarray(a, dtype=np.float32))
    x_prompt, x_sample = f(inputs["x_prompt"]), f(inputs["x_sample"])
    ckk, cvv, stt = f(inputs["cache_sb_k"])[0], f(inputs["cache_sb_v"])[0], f(inputs["state_gla"])[0]
    meta = f(inputs["meta_tokens"])
    gvec = np.concatenate([f(inputs["norm1_g"]).reshape(16, 128), f(inputs["norm2_g"]).reshape(16, 128),
                           f(inputs["norm_f_g"]).reshape(16, 128), f(inputs["gla_norm_g"]).reshape(16, 128)], 0)
    shared = {
        "w_in": f(inputs["w_in"])[0], "w_out": f(inputs["w_out"])[0], "w_up": f(inputs["w_up"])[0],
        "w_down": f(inputs["w_down"])[0], "w_al": f(inputs["w_alpha_up"])[0], "b_al": f(inputs["b_alpha"]).reshape(1, 1024),
        "gvec": np.ascontiguousarray(gvec), "gf_row": f(inputs["norm_f_g"]).reshape(1, D),
    }
    B = x_prompt.shape[0]
    NG_ALL = SEQ // 512
    NSP = 4 if NG_ALL % 4 == 0 else 1
    OWN = (NG_ALL // NSP) * 512
    E = OWN * (NSP - 1)
    T_P = N_META + SEQ
    core_bj = []
    in_maps = []
    for c in range(n_cores):
        b, j = (c // NSP, c % NSP) if NSP * B == n_cores else (c % B, 0)
        core_bj.append((b, j))
        m = dict(shared)
        m["xp"] = np.ascontiguousarray(np.concatenate(
            [np.zeros((128 - N_META + E - OWN * j, D), np.float32), meta, x_prompt[b][0:OWN * (j + 1)]], 0))
        m["xs"] = np.ascontiguousarray(x_sample[NS * c:NS * c + NS].reshape(NS * TS, D))
        m["ck"] = np.ascontiguousarray(ckk[NS * c:NS * c + NS].reshape(NS * PAST, D))
        m["cv"] = np.ascontiguousarray(cvv[NS * c:NS * c + NS].reshape(NS * PAST, D))
        m["st"] = np.ascontiguousarray(stt[NS * c:NS * c + NS].reshape(NS * H_GLA * DK, DV))
        in_maps.append(m)
    res = run_bass_kernel_spmd(nc, in_maps, core_ids=list(range(n_cores)))
    R = res.results
    yf = np.zeros((B, SEQ, D), np.float32)
    kf = np.zeros((B, T_P, D), np.float32)
    vf = np.zeros((B, T_P, D), np.float32)
    gla_p = np.zeros((1, B, H_GLA, DK, DV), np.float32)
    for c, (b, j) in enumerate(core_bj):
        if NSP * B != n_cores and c >= B:
            continue
        yf[b, OWN * j:OWN * (j + 1)] = R[c]["o_yp"]
        for dst, nm in ((kf, "o_kp"), (vf, "o_vp")):
            dst[b, N_META + OWN * j:N_META + OWN * (j + 1)] = R[c][nm][N_META:]
            if j == 0:
                dst[b, 0:N_META] = R[c][nm][0:N_META]
        if j == NSP - 1:
            gla_p[0, b] = R[c]["o_gp"].reshape(H_GLA, DK, DV)
    y_prompt = yf
    k_p = kf.reshape(1, B, T_P, H_SB, 128)
    v_p = vf.reshape(1, B, T_P, H_SB, 128)
    y_sample = np.concatenate([R[c]["o_ys"].reshape(NS, TS, D) for c in range(n_cores)], 0)
    gla_s = np.concatenate([R[c]["o_gs"].reshape(NS, H_GLA, DK, DV) for c in range(n_cores)], 0)[None]
    k_s = np.concatenate([R[c]["o_ks"].reshape(NS, TS, H_SB, 128) for c in range(n_cores)], 0)[None]
    v_s = np.concatenate([R[c]["o_vs"].reshape(NS, TS, H_SB, 128) for c in range(n_cores)], 0)[None]
    return (y_prompt, y_sample, gla_p, k_p, v_p, gla_s, k_s, v_s)


def kernel(**inputs):
    SEQ = int(np.asarray(inputs["x_prompt"]).shape[1])
    PAST = int(np.asarray(inputs["cache_sb_k"]).shape[2])
    return _run(inputs, SEQ, PAST)
```
